# Optimizing a Trainium2 kernel written in Bass

```python
import math
import jax, jax.numpy as jnp
from jax import lax
import numpy as np

D_MODEL = 4096
BATCH = 32
SEQ = 256
DEPTH = 1
DEC_BATCH = 8
DEC_SEQ = 4096
PAST_LEN = 512

GRID_W = 64
F_WIDTH = D_MODEL // 2
F_GROUPS = 4
F_GC = F_WIDTH // F_GROUPS
GLA_WIDTH = D_MODEL - F_WIDTH
GLA_HEADS = 4
GLA_DK_TOTAL = GLA_WIDTH // 2
GLA_DK = GLA_DK_TOTAL // GLA_HEADS
GLA_DV = GLA_WIDTH // GLA_HEADS
GLA_LOWRANK = 16
GLA_TAU = 16.0
GLA_CHUNK = 64
EPS = 1e-6
IN_SIZES = (F_WIDTH, F_WIDTH, GLA_DK_TOTAL, GLA_DK_TOTAL, GLA_WIDTH, GLA_WIDTH, GLA_LOWRANK, GLA_LOWRANK)
N_IN = sum(IN_SIZES)

kernel_name = "hymba_fnet_gla_diffusion_step"


def _split_offsets():
    offs, acc = [], 0
    for s in IN_SIZES[:-1]:
        acc += s
        offs.append(acc)
    return offs


def rmsnorm(x, w):
    xf = x.astype(jnp.float32)
    y = xf * lax.rsqrt(jnp.mean(xf * xf, axis=-1, keepdims=True) + EPS)
    return (y * w.astype(jnp.float32)).astype(x.dtype)


def fourier_mix(u, w_f, rows):
    B, L, _ = u.shape
    uf = u.astype(jnp.float32)
    if rows is None:
        spec = jnp.fft.fftn(uf.reshape(B, L, F_GROUPS, F_GC), axes=(1, 3), norm="ortho")
    else:
        spec = jnp.fft.fftn(uf.reshape(B, rows, GRID_W, F_GROUPS, F_GC), axes=(1, 2, 4), norm="ortho")
    re = jnp.real(spec).reshape(B, L, F_GROUPS, F_GC).astype(u.dtype)
    out = jnp.einsum('blgc,gcd->blgd', re, w_f)
    return out.reshape(B, L, F_WIDTH)


def gla_scan(q, k, v, g, s0):
    B, H, L, dk = q.shape
    dv = v.shape[-1]
    n = L // GLA_CHUNK

    def chunks(t):
        return jnp.moveaxis(t.reshape(B, H, n, GLA_CHUNK, t.shape[-1]), 2, 0)

    mask = jnp.tril(jnp.ones((GLA_CHUNK, GLA_CHUNK), dtype=bool))

    def step(S, inp):
        qc, kc, vc, gc = inp
        b = jnp.cumsum(gc, axis=-2)
        b_last = b[..., -1:, :]
        q_t = qc * jnp.exp(b)
        k_t = kc * jnp.exp(-b)
        o_inter = jnp.einsum('bhik,bhkv->bhiv', q_t, S)
        att = jnp.where(mask, jnp.einsum('bhik,bhjk->bhij', q_t, k_t), 0.0)
        o_intra = jnp.einsum('bhij,bhjv->bhiv', att, vc)
        k_dec = kc * jnp.exp(b_last - b)
        S_new = jnp.exp(b_last)[..., 0, :, None] * S + jnp.einsum('bhjk,bhjv->bhkv', k_dec, vc)
        return S_new.astype(S.dtype), (o_inter + o_intra).astype(vc.dtype)

    S_fin, o = lax.scan(step, s0, (chunks(q), chunks(k), chunks(v), chunks(g)))
    o = jnp.moveaxis(o, 0, 2).reshape(B, H, L, dv)
    return o, S_fin


def gla_bidir(q, k, v, g_f, g_b, s0_f, s0_b):
    o_f, s_f = gla_scan(q, k, v, g_f, s0_f)
    flip = lambda t: jnp.flip(t, axis=2)
    o_b, s_b = gla_scan(flip(q), flip(k), flip(v), flip(g_b), s0_b)
    return o_f + flip(o_b), s_f, s_b


def to_heads(t, d):
    B, L = t.shape[:2]
    return t.reshape(B, L, GLA_HEADS, d).transpose(0, 2, 1, 3)


def mixer_branch(h, w_in, w_alpha, b_alpha, w_fourier, gla_norm_w, w_out, rows, s0_f, s0_b):
    B, L, _ = h.shape
    proj = h @ w_in
    f_in, f_gate, q, k, v, g_gate, a_f, a_b = jnp.split(proj, _split_offsets(), axis=-1)
    f_out = fourier_mix(f_in, w_fourier, rows) * jax.nn.silu(f_gate)
    qh = to_heads(q, GLA_DK) * (GLA_DK ** -0.5)
    kh = to_heads(k, GLA_DK)
    vh = to_heads(v, GLA_DV)
    w32 = w_alpha.astype(jnp.float32)
    b32 = b_alpha.astype(jnp.float32)
    g_f = jax.nn.log_sigmoid(a_f.astype(jnp.float32) @ w32[0] + b32[0]) / GLA_TAU
    g_b = jax.nn.log_sigmoid(a_b.astype(jnp.float32) @ w32[1] + b32[1]) / GLA_TAU
    o, s_f, s_b = gla_bidir(qh, kh, vh, to_heads(g_f, GLA_DK), to_heads(g_b, GLA_DK), s0_f, s0_b)
    o = rmsnorm(o.transpose(0, 2, 1, 3), gla_norm_w)
    g_out = o.reshape(B, L, GLA_WIDTH) * jax.nn.silu(g_gate)
    out = jnp.concatenate([f_out, g_out], axis=-1) @ w_out
    return out, s_f, s_b


def setup_inputs(seed: int = 0) -> dict:
    key = jax.random.key(seed)
    ks = jax.random.split(key, 16)
    f32 = jnp.float32
    nrm = lambda k, shape, s: jax.random.normal(k, shape, f32) * s
    return {
        "x_prompt": nrm(ks[0], (BATCH, SEQ, D_MODEL), 1.0),
        "x_sample": nrm(ks[1], (DEC_BATCH, DEC_SEQ, D_MODEL), 1.0),
        "state_gla_fwd": nrm(ks[2], (DEC_BATCH, DEPTH, GLA_HEADS, GLA_DK, GLA_DV), 1.0),
        "state_gla_bwd": nrm(ks[3], (DEC_BATCH, DEPTH, GLA_HEADS, GLA_DK, GLA_DV), 1.0),
        "c": nrm(ks[4], (DEC_BATCH, D_MODEL), 1.0),
        "c_ctx": nrm(ks[5], (D_MODEL,), 1.0),
        "ada_w": nrm(ks[6], (DEPTH, D_MODEL, 3 * D_MODEL), D_MODEL ** -0.5),
        "ada_b": nrm(ks[7], (DEPTH, 3 * D_MODEL), 0.02),
        "norm_w": 1.0 + nrm(ks[8], (DEPTH, D_MODEL), 0.02),
        "w_in": nrm(ks[9], (DEPTH, D_MODEL, N_IN), D_MODEL ** -0.5),
        "w_alpha": nrm(ks[10], (DEPTH, 2, GLA_LOWRANK, GLA_DK_TOTAL), GLA_LOWRANK ** -0.5),
        "b_alpha": nrm(ks[11], (DEPTH, 2, GLA_DK_TOTAL), 0.5),
        "w_fourier": nrm(ks[12], (DEPTH, F_GROUPS, F_GC, F_GC), F_GC ** -0.5),
        "gla_norm_w": 1.0 + nrm(ks[13], (DEPTH, GLA_HEADS, GLA_DV), 0.02),
        "w_out": nrm(ks[14], (DEPTH, D_MODEL, D_MODEL), D_MODEL ** -0.5),
        "final_norm_w": 1.0 + nrm(ks[15], (D_MODEL,), 0.02),
    }


def reference(x_prompt, x_sample, state_gla_fwd, state_gla_bwd, c, c_ctx, ada_w, ada_b, norm_w,
              w_in, w_alpha, b_alpha, w_fourier, gla_norm_w, w_out, final_norm_w):
    Bp = x_prompt.shape[0]
    rows = x_sample.shape[1] // GRID_W
    zeros_state = jnp.zeros((Bp, GLA_HEADS, GLA_DK, GLA_DV), dtype=x_prompt.dtype)
    xp, xs = x_prompt, x_sample
    new_f, new_b = [], []
    for l in range(DEPTH):
        mod_ctx = jax.nn.silu(c_ctx) @ ada_w[l] + ada_b[l]
        sh, sc, gt = jnp.split(mod_ctx, 3, axis=-1)
        h = rmsnorm(xp, norm_w[l]) * (1.0 + sc) + sh
        out, s_f, s_b = mixer_branch(h, w_in[l], w_alpha[l], b_alpha[l], w_fourier[l], gla_norm_w[l],
                                     w_out[l], None, zeros_state, zeros_state)
        xp = xp + gt * out
        new_f.append(s_f)
        new_b.append(s_b)
        mod = jax.nn.silu(c) @ ada_w[l] + ada_b[l]
        sh, sc, gt = jnp.split(mod[:, None, :], 3, axis=-1)
        h = rmsnorm(xs, norm_w[l]) * (1.0 + sc) + sh
        out, _, _ = mixer_branch(h, w_in[l], w_alpha[l], b_alpha[l], w_fourier[l], gla_norm_w[l],
                                 w_out[l], rows, state_gla_fwd[:, l], state_gla_bwd[:, l])
        xs = xs + gt * out
    y_prompt = rmsnorm(xp, final_norm_w)
    y_sample = rmsnorm(xs, final_norm_w)
    new_state_fwd = jnp.stack(new_f, axis=1)
    new_state_bwd = jnp.stack(new_b, axis=1)
    return (y_prompt, y_sample, new_state_fwd, new_state_bwd)
```

```python
import numpy as np
from contextlib import ExitStack
import concourse.bass as bass
import concourse.mybir as mybir
from concourse.bass_utils import run_bass_kernel_spmd

F32 = mybir.dt.float32
BF16 = mybir.dt.bfloat16
AF = mybir.ActivationFunctionType
ALU = mybir.AluOpType

ENGS = ("sp", "act", "pe", "dve", "pool")
NDMA = 8

D = 4096
NIN = 10272
NMOD = 3 * D


class Res:
    __slots__ = ("name", "lw", "rd")

    def __init__(self, name=""):
        self.name = name
        self.lw = None
        self.rd = []


class Op:
    __slots__ = ("eng", "fn", "deps", "signal", "sigval", "dma", "dsem", "dval", "dprev", "phase")

    def __init__(self, eng, fn, dma):
        self.eng = eng
        self.fn = fn
        self.deps = []
        self.signal = False
        self.sigval = None
        self.dma = dma
        self.dsem = None
        self.dval = None
        self.dprev = None


class Sync:
    def __init__(self, nc, es):
        self.nc = nc
        self.sem = {e: es.enter_context(nc.semaphore("s_" + e)) for e in ENGS}
        self.cnt = {e: 0 for e in ENGS}
        self.dq = ("sp", "act", "pool")
        self.dsem = {q: [es.enter_context(nc.semaphore(f"d_{q}{i}")) for i in range(NDMA)] for q in self.dq}
        self.dtot = {q: [0] * NDMA for q in self.dq}
        self.dnext = {q: 0 for q in self.dq}
        self.nops = 0

    def clear_all(self):
        nc = self.nc
        with nc.Block() as block:
            @block.gpsimd
            def _(e):
                for s in list(self.sem.values()) + [x for v in self.dsem.values() for x in v]:
                    e.sem_clear(s)


class Phase:
    def __init__(self, sync):
        self.S = sync
        self.ops = []
        self.q = {e: [] for e in ENGS}

    def op(self, eng, fn, reads=(), writes=(), dma=False, ndma=1):
        o = Op(eng, fn, dma)
        o.phase = self
        deps = set()
        for r in reads:
            if r.lw is not None:
                deps.add(r.lw)
        for w in writes:
            if w.lw is not None:
                deps.add(w.lw)
            for k in w.rd:
                deps.add(k)
        for d in deps:
            if d.phase is not self:
                continue
            if d.dma:
                o.deps.append(d)
            else:
                if d.eng == "pe" and eng == "pe":
                    continue
                d.signal = True
                o.deps.append(d)
        if dma:
            S = self.S
            i = S.dnext[eng]
            S.dnext[eng] = (i + 1) % NDMA
            o.dsem = (eng, i)
            o.dprev = S.dtot[eng][i]
            S.dtot[eng][i] += 16 * ndma
            o.dval = S.dtot[eng][i]
        for r in reads:
            r.rd.append(o)
        for w in writes:
            w.lw = o
            w.rd = []
        self.ops.append(o)
        self.q[eng].append(o)
        return o

    def run(self):
        S = self.S
        nc = S.nc
        for e in ENGS:
            for o in self.q[e]:
                if o.signal and not o.dma:
                    S.cnt[e] += 1
                    o.sigval = S.cnt[e]
        S.nops += len(self.ops)

        def replay(ename, eng):
            waited = {}

            def wait(sem, key, val):
                if waited.get(key, 0) >= val:
                    return
                waited[key] = val
                eng.wait_ge(sem, val)

            for o in self.q[ename]:
                for d in o.deps:
                    if d.dma:
                        q, i = d.dsem
                        wait(S.dsem[q][i], ("d", q, i), d.dval)
                    else:
                        wait(S.sem[d.eng], ("e", d.eng), d.sigval)
                if o.dma:
                    q, i = o.dsem
                    if o.dprev > 0:
                        wait(S.dsem[q][i], ("d", q, i), o.dprev)
                    ins = o.fn(eng)
                    if not isinstance(ins, (list, tuple)):
                        ins = [ins]
                    for x in ins:
                        x.then_inc(S.dsem[q][i], 16)
                else:
                    ins = o.fn(eng)
                    if isinstance(ins, (list, tuple)):
                        ins = ins[-1]
                    if o.signal:
                        ins.then_inc(S.sem[ename], 1)
            if ename in S.dsem:
                for i in range(NDMA):
                    if S.dtot[ename][i] > 0:
                        wait(S.dsem[ename][i], ("d", ename, i), S.dtot[ename][i])

        with nc.Block() as block:
            @block.sync
            def _(e):
                replay("sp", e)

            @block.scalar
            def _(e):
                replay("act", e)

            @block.tensor
            def _(e):
                replay("pe", e)

            @block.vector
            def _(e):
                replay("dve", e)

            @block.gpsimd
            def _(e):
                replay("pool", e)


def make_consts():
    def cs(n, scale):
        i = np.arange(n)
        ang = 2.0 * np.pi * ((i[:, None] * i[None, :]) % n) / n
        return np.cos(ang) * scale, np.sin(ang) * scale

    C64, S64 = cs(64, 1.0 / 8.0)
    Z = np.zeros((64, 64))
    BDc = np.block([[C64, Z], [Z, C64]])
    BDs = np.block([[S64, Z], [Z, S64]])
    bdR = np.concatenate([BDc, BDs], axis=1)
    bdW = np.concatenate([BDc, BDs, -BDs, BDc], axis=1)
    CL, SL = cs(256, 1.0 / 16.0)
    dftL = np.concatenate([CL, SL], axis=1)
    Cc, Sc = cs(512, 512 ** -0.5)
    dftc = np.concatenate([Cc, -Sc], axis=1)
    j = np.arange(128)
    maskf = (j[:, None] <= j[None, :]).astype(np.float32)
    maskb = (j[:, None] >= j[None, :]).astype(np.float32)
    masks = np.stack([maskf, maskb], axis=1)
    f = lambda a: np.ascontiguousarray(a, dtype=np.float32)
    return {"ident": f(np.eye(128)), "bdR": f(bdR), "bdW": f(bdW), "dftL": f(dftL), "dftc": f(dftc),
            "masks": f(masks)}


def build(n_samp=4096, n_prompt=4, debug=False):
    NS = n_samp
    NP = n_prompt
    NT = NS + 256 * NP
    nc = bass.Bass("TRN2", target_bir_lowering=False)
    dbg_kind = "ExternalOutput" if debug else "Internal"

    def din(name, shape):
        return nc.dram_tensor(name, list(shape), F32, kind="ExternalInput").ap()

    def dout(name, shape):
        return nc.dram_tensor(name, list(shape), F32, kind="ExternalOutput").ap()

    def dscr(name, shape, dt=BF16):
        return nc.dram_tensor(name, list(shape), dt, kind=dbg_kind).ap()

    x_d = din("x", [NT, D])
    s0f_d = din("s0f", [4, 256, 512])
    s0b_d = din("s0b", [4, 256, 512])
    c2_d = din("c2", [2 * D])
    adaw_d = din("ada_w", [D, NMOD])
    adab_d = din("ada_b", [NMOD])
    normw_d = din("norm_w", [D])
    win_d = din("w_in", [D, NIN])
    walpha_d = din("w_alpha", [2, 16, 1024])
    balpha_d = din("b_alpha", [2 * 1024])
    wf_d = din("w_fourier", [4, 512, 512])
    gnw_d = din("gla_norm_w", [2048])
    wout_d = din("w_out", [D, D])
    fnw_d = din("final_norm_w", [D])
    ident_d = din("ident", [128, 128])
    bdR_d = din("bdR", [128, 256])
    bdW_d = din("bdW", [128, 512])
    dftL_d = din("dftL", [256, 512])
    dftc_d = din("dftc", [512, 1024])
    masks_d = din("masks", [128, 2, 128])

    y_d = dout("y", [NT, D])
    nsf_d = dout("nsf", [max(NP, 1), 4, 256, 512])
    nsb_d = dout("nsb", [max(NP, 1), 4, 256, 512])

    mod_d = dscr("mod_d", [2, NMOD], F32)
    winb_d = dscr("winb", [20, 128, 32 * 512])
    woutb_d = dscr("woutb", [8, 128, 32 * 512])
    mgb_d = dscr("mgb", [4, 128, 8 * 512])
    fin_d = dscr("fin", [NT, 2048])
    fgT_d = dscr("fgT", [2048, NT])
    qtT_d = dscr("qtT", [2, 1024, NT])
    ktT_d = dscr("ktT", [2, 1024, NT])
    kd_d = dscr("kd", [2, NT, 1024])
    v_d = dscr("v", [NT, 2048])
    gg_d = dscr("gg", [NT, 2048])
    pq1_d = dscr("pq1", [2, max(NS, 128), 2048])
    catT_d = dscr("catT", [D, NT])
    o_d = dscr("o_dir", [2, NT, 2048], F32)

    blocks = []
    for b in range(NS // 512):
        blocks.append((b * 512, 512, 0))
    t = NS
    rem = 256 * NP
    while rem > 0:
        n = min(512, rem)
        blocks.append((t, n, 1))
        t += n
        rem -= n
    seqs = []
    if NS:
        seqs.append((0, NS, 0, -1))
    for p in range(NP):
        seqs.append((NS + 256 * p, 256, 1, p))
    CH = 128
    NCH = NT // CH

    with ExitStack() as es:
        S = Sync(nc, es)
        S.clear_all()
        cur = [None]

        def sbt(st, name, shape, dt):
            return st.enter_context(nc.sbuf_tensor(name, list(shape), dt))

        def pst(st, name, shape, dt):
            return st.enter_context(nc.psum_tensor(name, list(shape), dt))

        def DMA(q, out, in_, r=(), w=()):
            cur[0].op(q, lambda e: e.dma_start(out=out, in_=in_), reads=r, writes=w, dma=True)

        def ACT(out, in_, func, r=(), w=(), **kw):
            cur[0].op("act", lambda e: e.activation(out=out, in_=in_, func=func, **kw), reads=r, writes=w)

        def TT(eng, out, a, b, op, r=(), w=()):
            cur[0].op(eng, lambda e: e.tensor_tensor(out=out, in0=a, in1=b, op=op), reads=r, writes=w)

        def STT(eng, out, in0, scalar, in1, op0, op1, r=(), w=()):
            cur[0].op(eng, lambda e: e.scalar_tensor_tensor(out=out, in0=in0, scalar=scalar, in1=in1, op0=op0, op1=op1),
                      reads=r, writes=w)

        def TS(eng, out, in0, s1, s2, op0, op1, r=(), w=()):
            cur[0].op(eng, lambda e: e.tensor_scalar(out=out, in0=in0, scalar1=s1, scalar2=s2, op0=op0, op1=op1),
                      reads=r, writes=w)

        def COPY(eng, out, in_, r=(), w=()):
            if eng == "act":
                cur[0].op("act", lambda e: e.activation(out=out, in_=in_, func=AF.Copy), reads=r, writes=w)
            else:
                cur[0].op(eng, lambda e: e.tensor_copy(out=out, in_=in_), reads=r, writes=w)

        def MEMSET(eng, ap, val, r=(), w=()):
            cur[0].op(eng, lambda e: e.memset(ap, val), reads=r, writes=w)

        def MMG(items, r=(), w=()):
            def f(e):
                n = len(items)
                for i, (o, l, rr) in enumerate(items):
                    ins = e.matmul(o, lhsT=l, rhs=rr, start=(i == 0), stop=(i == n - 1))
                return ins
            cur[0].op("pe", f, reads=r, writes=w)

        def TRS(items, ident, r=(), w=()):
            def f(e):
                for (o, i_) in items:
                    ins = e.transpose(out=o, in_=i_, identity=ident)
                return ins
            cur[0].op("pe", f, reads=r, writes=w)

        idf = sbt(es, "idf", [128, 128], F32)
        idb = sbt(es, "idb", [128, 128], BF16)
        ABt = sbt(es, "ABt", [128, 2, 2, 32], F32)
        modT = sbt(es, "modT", [128, 2, 96], F32)
        nbaT = sbt(es, "nbaT", [128, 16], F32)
        mid = ExitStack()
        walb = sbt(mid, "walb", [16, 2, 1024], BF16)
        wab = sbt(mid, "wab", [128, 32, 32], BF16)
        dec = sbt(mid, "dec", [128, 2, 8, NCH], F32)
        maskb16 = sbt(mid, "maskb16", [128, 2, 128], BF16)
        scanm = sbt(mid, "scanm", [128, 512], F32)
        r_id, r_AB, r_modT, r_nba, r_wal, r_wab, r_dec, r_mask, r_scanm = [Res() for _ in range(9)]
        r_mod_d, r_winb, r_woutb, r_mgb = Res(), [Res() for _ in range(20)], [Res() for _ in range(8)], [Res() for _ in range(4)]

        with ExitStack() as st:
            P = Phase(S)
            cur[0] = P
            c2t = sbt(st, "c2t", [64, 128], F32)
            sct = sbt(st, "sct", [64, 128], F32)
            scT2 = sbt(st, "scT2", [128, 32, 2], BF16)
            adab2 = [sbt(st, f"adab2_{i}", [2, 512], F32) for i in range(2)]
            modrow = [sbt(st, f"modrow{i}", [2, 512], F32) for i in range(2)]
            adaw = [sbt(st, f"adaw{i}", [128, 16, 512], F32) for i in range(3)]
            adawb = [sbt(st, f"adawb{i}", [128, 32, 512], BF16) for i in range(2)]
            r_adawb = [[Res(), Res()], [Res(), Res()]]
            mk_f = sbt(st, "mk_f", [128, 2, 128], F32)
            nwt = sbt(st, "nwt", [32, 128], F32)
            bat = sbt(st, "bat", [16, 128], F32)
            walf = sbt(st, "walf", [16, 2, 1024], F32)
            waf = sbt(st, "waf", [128, 32, 32], F32)
            m96 = [sbt(st, f"m96_{k}", [96, 128], F32) for k in range(2)]
            normT = sbt(st, "normT", [128, 32], F32)
            pmod = [pst(st, f"pmod{i}", [128, 512], F32) for i in range(2)]
            ptr = pst(st, "ptr0a", [128, 512], F32)
            r_c2, r_sct, r_scT2 = Res(), Res(), Res()
            r_adab2, r_modrow = [Res(), Res()], [Res(), Res()]
            r_adaw = [Res(), Res(), Res()]
            r_pmod = [Res(), Res()]
            r_ptr = Res()
            r_misc = Res()

            DMA("sp", idf[:, :], ident_d, w=[r_id])
            COPY("dve", idb[:, :], idf[:, :], r=[r_id], w=[r_id])
            DMA("sp", c2t[:, :], c2_d.rearrange("(a b) -> a b", b=128), w=[r_c2])
            ACT(sct[:, :], c2t[:, :], AF.Silu, r=[r_c2], w=[r_sct])
            TRS([(ptr[:, 0:64], sct[:, :])], idf[0:64, 0:64], r=[r_sct, r_id], w=[r_ptr])
            COPY("dve", scT2.ap().rearrange("p k m -> p m k"), ptr[:, 0:64].rearrange("p (m k) -> p m k", m=2),
                 r=[r_ptr], w=[r_scT2])
            adaw_v = adaw_d.rearrange("(c p) n -> p c n", p=128)
            NCB = NMOD // 512
            ak = 0
            for cb in range(NCB):
                b = cb % 2
                for hh in range(2):
                    fb = ak % 3
                    DMA("sp" if ak % 2 == 0 else "pool", adaw[fb][:, :, :], adaw_v[:, hh * 16:(hh + 1) * 16, cb * 512:(cb + 1) * 512], w=[r_adaw[fb]])
                    COPY("act" if ak % 2 else "dve", adawb[b][:, hh * 16:(hh + 1) * 16, :], adaw[fb][:, :, :],
                         r=[r_adaw[fb]], w=[r_adawb[b][hh]])
                    ak += 1
                DMA("sp", adab2[b][:, :], adab_d[cb * 512:(cb + 1) * 512].partition_broadcast(2), w=[r_adab2[b]])
                MMG([(pmod[b][0:2, :], scT2[:, kc, :], adawb[b][:, kc, :]) for kc in range(32)],
                    r=[r_scT2] + r_adawb[b], w=[r_pmod[b]])
                TT("dve", modrow[b][:, :], pmod[b][0:2, :], adab2[b][:, :], ALU.add,
                   r=[r_pmod[b], r_adab2[b]], w=[r_modrow[b]])
                DMA("act", mod_d[:, cb * 512:(cb + 1) * 512], modrow[b][:, :], r=[r_modrow[b]], w=[r_mod_d])
            for k in range(2):
                DMA("sp", m96[k][:, :], mod_d[k].rearrange("(a b) -> a b", b=128), r=[r_mod_d], w=[r_misc])
                TRS([(ptr[:, 0:96], m96[k][:, :])], idf[0:96, 0:96], r=[r_misc, r_id], w=[r_ptr])
                COPY("dve", modT[:, k, :], ptr[:, 0:96], r=[r_ptr], w=[r_modT])
            DMA("sp", nwt[:, :], normw_d.rearrange("(a b) -> a b", b=128), w=[r_misc])
            TRS([(ptr[:, 0:32], nwt[:, :])], idf[0:32, 0:32], r=[r_misc, r_id], w=[r_ptr])
            COPY("dve", normT[:, :], ptr[:, 0:32], r=[r_ptr], w=[r_misc])
            for k in range(2):
                STT("dve", ABt[:, k, 0, :], modT[:, k, 32:64], 1.0, normT[:, :], ALU.add, ALU.mult,
                    r=[r_modT, r_misc], w=[r_AB])
                COPY("dve", ABt[:, k, 1, :], modT[:, k, 0:32], r=[r_modT], w=[r_AB])
            DMA("sp", bat[:, :], balpha_d.rearrange("(a b) -> a b", b=128), w=[r_misc])
            TRS([(ptr[:, 0:16], bat[:, :])], idf[0:16, 0:16], r=[r_misc, r_id], w=[r_ptr])
            TS("dve", nbaT[:, :], ptr[:, 0:16], -1.0, 0.0, ALU.mult, ALU.add, r=[r_ptr], w=[r_nba])
            DMA("sp", walf[:, :, :], walpha_d.rearrange("d k n -> k d n"), w=[r_misc])
            COPY("dve", walb[:, :, :], walf[:, :, :], r=[r_misc], w=[r_wal])
            DMA("sp", waf[:, :, :], win_d.rearrange("(c p) n -> p c n", p=128)[:, :, 10240:10272], w=[r_misc])
            COPY("dve", wab[:, :, :], waf[:, :, :], r=[r_misc], w=[r_wab])
            DMA("sp", mk_f[:, :, :], masks_d, w=[r_misc])
            COPY("dve", maskb16[:, :, :], mk_f[:, :, :], r=[r_misc], w=[r_mask])
            MEMSET("pool", scanm[:, :], 1.0, w=[r_scanm])
            MEMSET("pool", scanm.ap().rearrange("p (c t) -> p c t", t=CH)[:, :, 0:1], 0.0, w=[r_scanm])
            P.run()

        with ExitStack() as st:
            P = Phase(S)
            cur[0] = P
            NB = 4
            wf32 = [sbt(st, f"wf32_{i}", [128, 16, 512], F32) for i in range(NB)]
            wb16 = [sbt(st, f"wb16_{i}", [128, 16 * 512], BF16) for i in range(NB)]
            r_wf32 = [Res() for _ in range(NB)]
            r_wb16 = [Res() for _ in range(NB)]
            k = 0
            cast_engs = ("dve", "act")
            for (src, dst, rdst, ncb) in ((win_d, winb_d, r_winb, 20),):
                sv = src.rearrange("(c p) n -> p c n", p=128)
                for cb in range(ncb):
                    for hh in range(2):
                        b = k % NB
                        DMA("sp" if k % 2 == 0 else "pool", wf32[b][:, :, :], sv[:, hh * 16:(hh + 1) * 16, cb * 512:(cb + 1) * 512],
                            w=[r_wf32[b]])
                        COPY(cast_engs[k % 2], wb16[b].ap().rearrange("p (c n) -> p c n", n=512), wf32[b][:, :, :],
                             r=[r_wf32[b]], w=[r_wb16[b]])
                        DMA("act", dst[cb][:, hh * 8192:(hh + 1) * 8192], wb16[b][:, :], r=[r_wb16[b]], w=[rdst[cb]])
                        k += 1
            P.run()
        with ExitStack() as st:
            P = Phase(S)
            cur[0] = P
            dcf = sbt(st, "dcf", [128, 4, 1024], F32)
            dcb = sbt(st, "dcb", [128, 4, 1024], BF16)
            wff = sbt(st, "wff", [128, 4, 512], F32)
            wfb = [sbt(st, f"wfb{i}", [128, 4, 512], BF16) for i in range(2)]
            mgt = [sbt(st, f"mgt{i}", [128, 8, 512], BF16) for i in range(2)]
            pmg = [pst(st, f"pmg{i}", [128, 512], F32) for i in range(2)]
            r_dc, r_wff, r_wfb, r_mgt, r_pmg = Res(), Res(), [Res(), Res()], [Res(), Res()], [Res(), Res()]
            DMA("sp", dcf[:, :, :], dftc_d.rearrange("(c p) n -> p c n", p=128), w=[r_dc])
            COPY("dve", dcb[:, :, :], dcf[:, :, :], r=[r_dc], w=[r_dc])
            kk = 0
            for g in range(4):
                DMA("sp", wff[:, :, :], wf_d[g].rearrange("(c p) n -> p c n", p=128), w=[r_wff])
                COPY("dve", wfb[g % 2][:, :, :], wff[:, :, :], r=[r_wff], w=[r_wfb[g % 2]])
                for kc in range(8):
                    pb = kk % 2
                    kk += 1
                    MMG([(pmg[pb][:, :], dcb[:, cc, kc * 128:(kc + 1) * 128], wfb[g % 2][:, cc, :]) for cc in range(4)],
                        r=[r_dc, r_wfb[g % 2]], w=[r_pmg[pb]])
                    COPY("act" if kc % 2 else "dve", mgt[g % 2][:, kc, :], pmg[pb][:, :], r=[r_pmg[pb]], w=[r_mgt[g % 2]])
                DMA("sp", mgb_d[g], mgt[g % 2].ap().rearrange("p k n -> p (k n)"), r=[r_mgt[g % 2]], w=[r_mgb[g]])
            P.run()

        r_fin, r_fgT, r_v, r_gg, r_catT, r_pq1 = [Res() for _ in range(6)]
        r_qtT, r_ktT, r_kd = [Res(), Res()], [Res(), Res()], [Res(), Res()]

        with ExitStack() as st:
            P = Phase(S)
            cur[0] = P
            hT = sbt(st, "hT", [128, 32, 512], BF16)
            Wt = [sbt(st, f"Wt{i}", [128, 32, 512], BF16) for i in range(2)]
            xts = [sbt(st, f"xt{i}", [128, D], F32) for i in range(2)]
            r_xts = [Res(), Res()]
            xti = [0]
            xn = sbt(st, "xn", [128, D], BF16)
            qT = sbt(st, "qT", [128, 8, 512], BF16)
            kT = sbt(st, "kT", [128, 8, 512], BF16)
            kdT = sbt(st, "kdT", [128, 8, 512], BF16)
            kdtm = sbt(st, "kdtm", [128, 4, 1024], BF16)
            stg = [sbt(st, f"stg{i}", [128, 4, 512], BF16) for i in range(2)]
            aT = sbt(st, "aT", [16, 2, 512], BF16)
            sst = sbt(st, "sst", [128, 4], F32)
            gl = [sbt(st, f"gl{i}", [128, 512], F32) for i in range(1)]
            gc = [sbt(st, f"gc{i}", [128, 512], F32) for i in range(1)]
            gd = [sbt(st, f"gd{i}", [128, 512], F32) for i in range(1)]
            ge = [sbt(st, f"ge{i}", [128, 512], F32) for i in range(3)]
            sq = [sbt(st, f"sq{i}", [128, 512], BF16) for i in range(2)]
            sk = [sbt(st, f"sk{i}", [128, 512], BF16) for i in range(2)]
            pT = [pst(st, f"pT{i}", [128, 1024], BF16) for i in range(2)]
            pm = [pst(st, f"pm{i}", [128, 512], F32) for i in range(4)]
            pg = [pst(st, f"pg{i}", [128, 512], F32) for i in range(2)]
            r_hT, r_xn, r_qT, r_kT, r_kdT, r_kdtm, r_aT, r_ss = [Res() for _ in range(8)]
            r_Wt = [Res(), Res()]
            r_stg = [Res(), Res()]
            r_gl, r_gc, r_gd = [Res(), Res()], [Res(), Res()], [Res(), Res()]
            r_ge = [Res() for _ in range(3)]
            r_sq, r_sk = [Res(), Res()], [Res(), Res()]
            r_pT, r_pm, r_pg = [Res(), Res()], [Res() for _ in range(4)], [Res(), Res()]
            wload = [0]
            pmi = [0]
            gi = [0]
            evi = [0]

            def load_W(cb):
                b = wload[0] % 2
                wload[0] += 1
                DMA("sp", Wt[b].ap().rearrange("p c n -> p (c n)"), winb_d[cb], r=[r_winb[cb]], w=[r_Wt[b]])
                return b

            nblk = len(blocks)
            wb_next = load_W(8)
            for bi, (t0, TB, kind) in enumerate(blocks):
                NTT = TB // 128
                nchb = TB // CH
                ch0 = t0 // CH
                for tt in range(NTT):
                    xt = xts[xti[0] % 2]
                    r_xt = r_xts[xti[0] % 2]
                    xti[0] += 1
                    DMA("pool", xt[:, :], x_d[t0 + tt * 128:t0 + (tt + 1) * 128, :], w=[r_xt])
                    ACT(xn[:, :], xt[:, :], AF.Square, r=[r_xt], w=[r_xn, r_ss], accum_out=sst[:, 0:1])
                    TS("dve", sst[:, 1:2], sst[:, 0:1], 1.0 / D, 1e-6, ALU.mult, ALU.add, r=[r_ss], w=[r_ss])
                    cur[0].op("act", lambda e: e.sqrt(out=sst[:, 2:3], in_=sst[:, 1:2]), reads=[r_ss], writes=[r_ss])
                    cur[0].op("dve", lambda e: e.reciprocal(out=sst[:, 3:4], in_=sst[:, 2:3]), reads=[r_ss], writes=[r_ss])
                    TS("dve", xn[:, 0:2048], xt[:, 0:2048], sst[:, 3:4], 0.0, ALU.mult, ALU.add, r=[r_xt, r_ss], w=[r_xn])
                    ACT(xn[:, 2048:4096], xt[:, 2048:4096], AF.Copy, r=[r_xt, r_ss], w=[r_xn], scale=sst[:, 3:4])
                    for g8 in range(4):
                        pb = g8 % 2
                        TRS([(pT[pb][:, i * 128:(i + 1) * 128], xn[:, (g8 * 8 + i) * 128:(g8 * 8 + i + 1) * 128]) for i in range(8)],
                            idb[:, :], r=[r_xn, r_id], w=[r_pT[pb]])

                        def f_ev(e, g8=g8, pb=pb, tt=tt, kind=kind, act=(g8 % 2 == 0)):
                            for i in range(8):
                                c = g8 * 8 + i
                                if act:
                                    ins = e.activation(out=hT[:, c, tt * 128:(tt + 1) * 128], in_=pT[pb][:, i * 128:(i + 1) * 128],
                                                       func=AF.Identity, scale=ABt[:, kind, 0, c:c + 1], bias=ABt[:, kind, 1, c:c + 1])
                                else:
                                    ins = e.tensor_scalar(out=hT[:, c, tt * 128:(tt + 1) * 128], in0=pT[pb][:, i * 128:(i + 1) * 128],
                                                          scalar1=ABt[:, kind, 0, c:c + 1], scalar2=ABt[:, kind, 1, c:c + 1],
                                                          op0=ALU.mult, op1=ALU.add)
                            return ins
                        cur[0].op("act" if g8 % 2 == 0 else "dve", f_ev, reads=[r_pT[pb], r_AB], writes=[r_hT])

                units = []

                def unit_a(dr, TB=TB):
                    pb = gi[0] % 2
                    gi[0] += 1
                    MMG([(pg[pb][0:16, 0:TB], wab[:, kc, dr * 16:(dr + 1) * 16], hT[:, kc, 0:TB]) for kc in range(32)],
                        r=[r_hT, r_wab], w=[r_pg[pb]])
                    COPY("dve", aT[:, dr, 0:TB], pg[pb][0:16, 0:TB], r=[r_pg[pb]], w=[r_aT])

                def unit_g(dr, c8, TB=TB, nchb=nchb, ch0=ch0, t0=t0):
                    pb = gi[0] % 2
                    gi[0] += 1
                    i2 = 0
                    i3 = c8 % 2
                    MMG([(pg[pb][:, 0:TB], walb[:, dr, c8 * 128:(c8 + 1) * 128], aT[:, dr, 0:TB])],
                        r=[r_wal, r_aT], w=[r_pg[pb]])
                    ACT(gl[i2][:, 0:TB], pg[pb][:, 0:TB], AF.Exp, r=[r_pg[pb], r_nba], w=[r_gl[i2]],
                        scale=-1.0, bias=nbaT[:, dr * 8 + c8:dr * 8 + c8 + 1])
                    ACT(gl[i2][:, 0:TB], gl[i2][:, 0:TB], AF.Ln, r=[r_gl[i2]], w=[r_gl[i2]], bias=1.0)
                    cur[0].op("dve", lambda e, i2=i2, TB=TB: e.tensor_tensor_scan(
                        out=gc[i2][:, 0:TB], data0=scanm[:, 0:TB], data1=gl[i2][:, 0:TB], initial=0.0,
                        op0=ALU.mult, op1=ALU.add), reads=[r_gl[i2], r_scanm], writes=[r_gc[i2]])
                    clv = gc[i2][:, 0:TB].rearrange("p (c t) -> p c t", t=CH)
                    dv_ = gd[i2][:, 0:TB].rearrange("p (c t) -> p c t", t=CH)
                    totb = clv[:, :, CH - 1:CH].to_broadcast([128, nchb, CH])
                    TT("dve", dv_, totb, clv, ALU.subtract, r=[r_gc[i2]], w=[r_gd[i2]])
                    if dr == 0:
                        dq_ap, dq_r = gc[i2][:, 0:TB], r_gc[i2]
                        dkd_ap, dkd_r = gd[i2][:, 0:TB], r_gd[i2]
                    else:
                        TT("pool", gd[i2][:, 0:TB], gd[i2][:, 0:TB], gl[i2][:, 0:TB], ALU.add,
                           r=[r_gd[i2], r_gl[i2]], w=[r_gd[i2]])
                        TT("pool", gl[i2][:, 0:TB], gc[i2][:, 0:TB], gl[i2][:, 0:TB], ALU.subtract,
                           r=[r_gc[i2], r_gl[i2]], w=[r_gl[i2]])
                        dq_ap, dq_r = gd[i2][:, 0:TB], r_gd[i2]
                        dkd_ap, dkd_r = gl[i2][:, 0:TB], r_gl[i2]
                    ACT(dec[:, dr, c8, ch0:ch0 + nchb], gc[i2][:, 0:TB].rearrange("p (c t) -> p c t", t=CH)[:, :, CH - 1],
                        AF.Exp, r=[r_gc[i2]], w=[r_dec], scale=-1.0 / 16.0)
                    ACT(ge[0][:, 0:TB], dq_ap, AF.Exp, r=[dq_r], w=[r_ge[0]], scale=-1.0 / 16.0)
                    TT("dve", sq[i3][:, 0:TB], qT[:, c8, 0:TB], ge[0][:, 0:TB], ALU.mult, r=[r_qT, r_ge[0]], w=[r_sq[i3]])
                    DMA("sp", qtT_d[dr, c8 * 128:(c8 + 1) * 128, t0:t0 + TB], sq[i3][:, 0:TB], r=[r_sq[i3]], w=[r_qtT[dr]])
                    ACT(ge[1][:, 0:TB], dq_ap, AF.Exp, r=[dq_r], w=[r_ge[1]], scale=1.0 / 16.0)
                    TT("pool", sk[i3][:, 0:TB], kT[:, c8, 0:TB], ge[1][:, 0:TB], ALU.mult, r=[r_kT, r_ge[1]], w=[r_sk[i3]])
                    DMA("sp", ktT_d[dr, c8 * 128:(c8 + 1) * 128, t0:t0 + TB], sk[i3][:, 0:TB], r=[r_sk[i3]], w=[r_ktT[dr]])
                    ACT(ge[2][:, 0:TB], dkd_ap, AF.Exp, r=[dkd_r], w=[r_ge[2]], scale=-1.0 / 16.0)
                    TT("dve", kdT[:, c8, 0:TB], kT[:, c8, 0:TB], ge[2][:, 0:TB], ALU.mult, r=[r_kT, r_ge[2]], w=[r_kdT])

                def unit_kd(dr, TB=TB, NTT=NTT, t0=t0):
                    for tt in range(NTT):
                        pb = tt % 2
                        TRS([(pT[pb][:, c8 * 128:(c8 + 1) * 128], kdT[:, c8, tt * 128:(tt + 1) * 128]) for c8 in range(8)],
                            idb[:, :], r=[r_kdT, r_id], w=[r_pT[pb]])
                        COPY("act" if tt % 2 else "dve", kdtm[:, tt, :], pT[pb][:, :], r=[r_pT[pb]], w=[r_kdtm])
                    DMA("sp", kd_d[dr, t0:t0 + TB, :].rearrange("(t p) n -> p t n", p=128), kdtm[:, 0:NTT, :],
                        r=[r_kdtm], w=[r_kd[dr]])

                for dr in range(2):
                    for c8 in range(8):
                        units.append(lambda dr=dr, c8=c8: unit_g(dr, c8))
                    units.append(lambda dr=dr: unit_kd(dr))

                cb_order = [8, 9, 10, 11] + [0, 1, 2, 3, 4, 5, 6, 7, 12, 13, 14, 15, 16, 17, 18, 19]
                for ci_, cb in enumerate(cb_order):
                    wb = wb_next
                    if ci_ + 1 < 20:
                        wb_next = load_W(cb_order[ci_ + 1])
                    elif bi + 1 < nblk:
                        wb_next = load_W(cb_order[0])
                    tokmajor = cb < 4 or cb >= 12
                    sb_ = evi[0] % 2
                    isqk = cb in (8, 9, 10, 11)
                    if not isqk:
                        evi[0] += 1
                    n_it = NTT if tokmajor else 4
                    for it in range(n_it):
                        pb = pmi[0] % 4
                        pmi[0] += 1
                        if tokmajor:
                            MMG([(pm[pb][:, :], hT[:, kc, it * 128:(it + 1) * 128], Wt[wb][:, kc, :]) for kc in range(32)],
                                r=[r_hT, r_Wt[wb]], w=[r_pm[pb]])
                        else:
                            MMG([(pm[pb][:, 0:TB], Wt[wb][:, kc, it * 128:(it + 1) * 128], hT[:, kc, 0:TB]) for kc in range(32)],
                                r=[r_hT, r_Wt[wb]], w=[r_pm[pb]])
                        ev = "act" if (pmi[0] % 2) else "dve"
                        if cb in (8, 9):
                            c8 = (cb - 8) * 4 + it
                            if ev == "act":
                                ACT(qT[:, c8, 0:TB], pm[pb][:, 0:TB], AF.Copy, r=[r_pm[pb]], w=[r_qT], scale=1.0 / 16.0)
                            else:
                                TS("dve", qT[:, c8, 0:TB], pm[pb][:, 0:TB], 1.0 / 16.0, 0.0, ALU.mult, ALU.add,
                                   r=[r_pm[pb]], w=[r_qT])
                        elif cb in (10, 11):
                            c8 = (cb - 10) * 4 + it
                            COPY(ev, kT[:, c8, 0:TB], pm[pb][:, 0:TB], r=[r_pm[pb]], w=[r_kT])
                        elif tokmajor:
                            COPY(ev, stg[sb_][:, it, :], pm[pb][:, :], r=[r_pm[pb]], w=[r_stg[sb_]])
                        else:
                            COPY(ev, stg[sb_][:, it, 0:TB], pm[pb][:, 0:TB], r=[r_pm[pb]], w=[r_stg[sb_]])
                    if cb < 4:
                        DMA("sp", fin_d[t0:t0 + TB, cb * 512:(cb + 1) * 512].rearrange("(t p) n -> p t n", p=128),
                            stg[sb_][:, 0:NTT, :], r=[r_stg[sb_]], w=[r_fin])
                    elif cb < 8:
                        DMA("sp", fgT_d[(cb - 4) * 512:(cb - 3) * 512, t0:t0 + TB].rearrange("(j p) n -> p j n", p=128),
                            stg[sb_][:, :, 0:TB], r=[r_stg[sb_]], w=[r_fgT])
                    elif cb >= 16:
                        DMA("sp", gg_d[t0:t0 + TB, (cb - 16) * 512:(cb - 15) * 512].rearrange("(t p) n -> p t n", p=128),
                            stg[sb_][:, 0:NTT, :], r=[r_stg[sb_]], w=[r_gg])
                    elif cb >= 12:
                        DMA("sp", v_d[t0:t0 + TB, (cb - 12) * 512:(cb - 11) * 512].rearrange("(t p) n -> p t n", p=128),
                            stg[sb_][:, 0:NTT, :], r=[r_stg[sb_]], w=[r_v])
                    if ci_ == 3:
                        unit_a(0)
                        unit_a(1)
                    elif ci_ > 3:
                        nu = 2 if (ci_ - 4) < 2 else 1
                        for _ in range(nu):
                            if units:
                                units.pop(0)()
                while units:
                    units.pop(0)()
            P.run()

        with ExitStack() as st:
            P = Phase(S)
            cur[0] = P
            mg = sbt(st, "mg", [128, 4, 8 * 512], BF16)
            cf = sbt(st, "cf", [128, 1280], F32)
            bdRb = sbt(st, "bdRb", [128, 256], BF16)
            bdWb = sbt(st, "bdWb", [128, 512], BF16)
            dLb = sbt(st, "dLb", [128, 2, 512], BF16)
            uin = [sbt(st, f"uin{i}", [128, 2048], BF16) for i in range(2)]
            pqs = [sbt(st, f"pqs{i}", [128, 2, 2048], BF16) for i in range(2)]
            pq1t = sbt(st, "pq1t", [128, 4, 2, 2048], BF16)
            PQT = sbt(st, "PQT", [128, 2, 16, 512], BF16)
            sfg = sbt(st, "sfg", [128, 16, 512], BF16)
            cst = [sbt(st, f"cst{i}", [128, 4, 512], BF16) for i in range(2)]
            pp = [pst(st, f"pp{i}", [128, 512], F32) for i in range(8)]
            r_mg, r_cf, r_bd, r_PQT, r_sfg, r_pq1t = [Res() for _ in range(6)]
            r_uin, r_pqs, r_cst = [Res(), Res(), Res()], [Res(), Res(), Res()], [Res(), Res()]
            r_pp = [Res() for _ in range(8)]
            ppi = [0]
            wof = [sbt(st, f"wof{i}", [128, 8, 512], F32) for i in range(2)]
            wob = [sbt(st, "wob0", [128, 8 * 512], BF16)] * 2
            r_wof = [Res(), Res()]
            _rw = Res()
            r_wob = [_rw, _rw]
            wo_v = wout_d.rearrange("(c p) n -> p c n", p=128)
            wo_jobs = [(cb, hh) for cb in range(8) for hh in range(4)]
            wo_pend = []
            wo_k = [0]

            def wo_job():
                if wo_pend:
                    cb, hh, b = wo_pend.pop(0)
                    COPY("act" if hh % 2 else "dve", wob[b].ap().rearrange("p (c n) -> p c n", n=512), wof[b][:, :, :],
                         r=[r_wof[b]], w=[r_wob[b]])
                    DMA("pool", woutb_d[cb][:, hh * 4096:(hh + 1) * 4096], wob[b][:, :], r=[r_wob[b]], w=[r_woutb[cb]])
                if wo_jobs:
                    cb, hh = wo_jobs.pop(0)
                    b = wo_k[0] % 2
                    wo_k[0] += 1
                    DMA("sp", wof[b][:, :, :], wo_v[:, hh * 8:(hh + 1) * 8, cb * 512:(cb + 1) * 512], w=[r_wof[b]])
                    wo_pend.append((cb, hh, b))

            def nextpp():
                i = ppi[0] % 8
                ppi[0] += 1
                return i

            for g in range(4):
                DMA("sp", mg[:, g, :], mgb_d[g], r=[r_mgb[g]], w=[r_mg])
            DMA("sp", cf[:, 0:256], bdR_d, w=[r_cf])
            DMA("sp", cf[:, 256:768], bdW_d, w=[r_cf])
            COPY("dve", bdRb[:, :], cf[:, 0:256], r=[r_cf], w=[r_bd])
            COPY("dve", bdWb[:, :], cf[:, 256:768], r=[r_cf], w=[r_bd])
            for lc in range(2):
                DMA("sp", cf[:, 768:1280], dftL_d[lc * 128:(lc + 1) * 128, :], r=[], w=[r_cf])
                COPY("dve", dLb[:, lc, :], cf[:, 768:1280], r=[r_cf], w=[r_bd])
            mgv = mg.ap().rearrange("p g (k n) -> p g k n", n=512)

            def channel_stage(t0, TB):
                DMA("sp", sfg[:, :, 0:TB], fgT_d[:, t0:t0 + TB].rearrange("(j p) n -> p j n", p=128), r=[r_fgT], w=[r_sfg])
                ACT(sfg[:, :, 0:TB], sfg[:, :, 0:TB], AF.Silu, r=[r_sfg], w=[r_sfg])
                for g in range(4):
                    cb_ = g % 2
                    for dch in range(4):
                        pi = nextpp()
                        MMG([(pp[pi][:, 0:TB], mgv[:, g, kc, dch * 128:(dch + 1) * 128], PQT[:, kc // 4, g * 4 + kc % 4, 0:TB])
                             for kc in range(8)], r=[r_mg, r_PQT], w=[r_pp[pi]])
                        TT("dve", cst[cb_][:, dch, 0:TB], pp[pi][:, 0:TB], sfg[:, g * 4 + dch, 0:TB], ALU.mult,
                           r=[r_pp[pi], r_sfg], w=[r_cst[cb_]])
                    DMA("pool", catT_d[g * 512:(g + 1) * 512, t0:t0 + TB].rearrange("(j p) n -> p j n", p=128),
                        cst[cb_][:, :, 0:TB], r=[r_cst[cb_]], w=[r_catT])

            for (s0, sn, kind, pidx) in seqs:
                if kind == 0:
                    finv = fin_d[s0:s0 + sn, :].rearrange("(r w) c -> w r c", w=64)
                    for wp in range(32):
                        ub = wp % 2
                        wo_job()
                        for wl in range(2):
                            DMA("sp", uin[ub][wl * 64:(wl + 1) * 64, :], finv[wp * 2 + wl], r=[r_fin], w=[r_uin[ub]])
                        for g in range(4):
                            for pq in range(2):
                                pi = nextpp()
                                MMG([(pp[pi][:, :], bdRb[:, pq * 128:(pq + 1) * 128], uin[ub][:, g * 512:(g + 1) * 512])],
                                    r=[r_bd, r_uin[ub]], w=[r_pp[pi]])
                                COPY("act" if pq else "dve", pqs[ub][:, pq, g * 512:(g + 1) * 512], pp[pi][:, :],
                                     r=[r_pp[pi]], w=[r_pqs[ub]])
                        for pq in range(2):
                            pv = pq1_d[pq, 0:sn, :].rearrange("(r w) c -> w r c", w=64)
                            for wl in range(2):
                                DMA("pool", pv[wp * 2 + wl], pqs[ub][wl * 64:(wl + 1) * 64, pq, :], r=[r_pqs[ub]], w=[r_pq1])
                    for b in range(sn // 512):
                        t0 = s0 + b * 512
                        for pq in range(2):
                            DMA("sp", pq1t[:, :, pq, :], pq1_d[pq, b * 512:(b + 1) * 512, :].rearrange("(t p) c -> p t c", p=128),
                                r=[r_pq1], w=[r_pq1t])
                        for tt in range(4):
                            for c16 in range(16):
                                pi = nextpp()
                                MMG([(pp[pi][:, 0:256], pq1t[:, tt, 0, c16 * 128:(c16 + 1) * 128], bdWb[:, 0:256]),
                                     (pp[pi][:, 0:256], pq1t[:, tt, 1, c16 * 128:(c16 + 1) * 128], bdWb[:, 256:512])],
                                    r=[r_bd, r_pq1t], w=[r_pp[pi]])
                                COPY("act" if c16 % 2 else "dve", PQT[:, :, c16, tt * 128:(tt + 1) * 128],
                                     pp[pi][:, 0:256].rearrange("p (a n) -> p a n", a=2), r=[r_pp[pi]], w=[r_PQT])
                        channel_stage(t0, 512)
                else:
                    for lc in range(2):
                        DMA("sp", pq1t[:, lc, 0, :], fin_d[s0 + lc * 128:s0 + (lc + 1) * 128, :], r=[r_fin], w=[r_pq1t])
                    for c16 in range(16):
                        pi = nextpp()
                        MMG([(pp[pi][:, :], pq1t[:, lc, 0, c16 * 128:(c16 + 1) * 128], dLb[:, lc, :]) for lc in range(2)],
                            r=[r_bd, r_pq1t], w=[r_pp[pi]])
                        COPY("act" if c16 % 2 else "dve", PQT[:, :, c16, 0:256],
                             pp[pi][:, :].rearrange("p (a n) -> p a n", a=2), r=[r_pp[pi]], w=[r_PQT])
                    channel_stage(s0, 256)
            while wo_jobs or wo_pend:
                wo_job()
            P.run()

        r_o = [Res(), Res()]
        with ExitStack() as st:
            P = Phase(S)
            cur[0] = P
            Sst = sbt(st, "Sst", [128, 8, 512], F32)
            Sbf = sbt(st, "Sbf", [128, 8, 512], BF16)
            qb = [sbt(st, f"qb{i}", [128, 8, 512], BF16) for i in range(2)]
            kb = [sbt(st, f"kb{i}", [128, 8, 512], BF16) for i in range(2)]
            kdb = [sbt(st, f"kdb{i}", [128, 4, 1024], BF16) for i in range(2)]
            vb = [sbt(st, f"vb{i}", [128, 4, 2048], BF16) for i in range(2)]
            attm = [sbt(st, f"attm{i}", [128, 4, 128], BF16) for i in range(2)]
            ofs = [sbt(st, f"ofs{i}", [128, 2048], F32) for i in range(2)]
            pa = [pst(st, f"pa{i}", [128, 512], F32) for i in range(2)]
            po = [pst(st, f"po{i}", [128, 512], F32) for i in range(3)]
            pz = [pst(st, f"pz{i}", [128, 512], F32) for i in range(3)]
            r_S = [Res() for _ in range(8)]
            r_Sbf = [Res() for _ in range(8)]
            r_qb, r_kb, r_kdb, r_vb, r_attm, r_ofs, r_pa = [[Res(), Res()] for _ in range(7)]
            r_po, r_pz = [Res() for _ in range(3)], [Res() for _ in range(3)]
            cnt = {"po": 0, "pz": 0, "pa": 0, "ld": 0, "of": 0, "cast": 0}

            jobs = [(s0, sn, kind, pidx, dr) for (s0, sn, kind, pidx) in seqs for dr in range(2)]
            ctxs = []
            for (s0, sn, kind, pidx, dr) in jobs:
                nch = sn // CH
                GC = min(4, nch)
                order = list(range(nch)) if dr == 0 else list(range(nch - 1, -1, -1))
                ctxs.append({"s0": s0, "nch": nch, "GC": GC, "order": order, "ngb": nch // GC, "lb_of": {}, "dr": dr})

            def load_gb(cx, gbi):
                GC, dr, s0 = cx["GC"], cx["dr"], cx["s0"]
                chs = cx["order"][gbi * GC:(gbi + 1) * GC]
                c_lo = min(chs)
                tk = s0 + c_lo * CH
                ng = GC * CH
                b = cnt["ld"] % 2
                cnt["ld"] += 1
                DMA("sp", qb[b][:, :, 0:ng], qtT_d[dr, :, tk:tk + ng].rearrange("(c p) n -> p c n", p=128),
                    r=[r_qtT[dr]], w=[r_qb[b]])
                DMA("sp", kb[b][:, :, 0:ng], ktT_d[dr, :, tk:tk + ng].rearrange("(c p) n -> p c n", p=128),
                    r=[r_ktT[dr]], w=[r_kb[b]])
                DMA("sp", kdb[b][:, 0:GC, :], kd_d[dr, tk:tk + ng, :].rearrange("(c p) n -> p c n", p=128),
                    r=[r_kd[dr]], w=[r_kdb[b]])
                DMA("sp", vb[b][:, 0:GC, :], v_d[tk:tk + ng, :].rearrange("(c p) n -> p c n", p=128),
                    r=[r_v], w=[r_vb[b]])
                cx["lb_of"][gbi] = (b, c_lo)

            def emit_att(cx, n_i):
                ch = cx["order"][n_i]
                b, c_lo = cx["lb_of"][n_i // cx["GC"]]
                cl_ = ch - c_lo
                ab = cnt["pa"] % 2
                cnt["pa"] += 1
                for h in range(4):
                    MMG([(pa[ab][:, h * 128:(h + 1) * 128], kb[b][:, h * 2 + dk, cl_ * CH:(cl_ + 1) * CH],
                          qb[b][:, h * 2 + dk, cl_ * CH:(cl_ + 1) * CH]) for dk in range(2)],
                        r=[r_kb[b], r_qb[b]], w=[r_pa[ab]])
                return ab

            load_gb(ctxs[0], 0)
            for ji, (s0, sn, kind, pidx, dr) in enumerate(jobs):
                cx = ctxs[ji]
                nch, GC, order, ngb = cx["nch"], cx["GC"], cx["order"], cx["ngb"]
                if kind == 0:
                    src = (s0f_d if dr == 0 else s0b_d).rearrange("h (c p) v -> p (h c) v", p=128)
                    DMA("pool", Sst[:, :, :], src, w=r_S)
                else:
                    MEMSET("pool", Sst[:, :, :], 0.0, w=r_S)
                for i in range(8):
                    COPY("act" if i % 2 else "dve", Sbf[:, i, :], Sst[:, i, :], r=[r_S[i]], w=[r_Sbf[i]])
                ab_next = emit_att(cx, 0)
                for n_i in range(nch):
                    ch = order[n_i]
                    gch = s0 // CH + ch
                    tk = s0 + ch * CH
                    b, c_lo = cx["lb_of"][n_i // GC]
                    cl_ = ch - c_lo
                    ab = ab_next
                    if n_i % GC == 0 and n_i // GC + 1 < ngb:
                        load_gb(cx, n_i // GC + 1)
                    if n_i == nch - 1 and ji + 1 < len(jobs):
                        load_gb(ctxs[ji + 1], 0)
                    TT("dve", attm[ab][:, :, :], pa[ab][:, :].rearrange("p (h n) -> p h n", h=4),
                       maskb16[:, dr:dr + 1, :].to_broadcast([128, 4, 128]), ALU.mult, r=[r_pa[ab], r_mask], w=[r_attm[ab]])
                    if n_i + 1 < nch:
                        ab_next = emit_att(cx, n_i + 1)
                    ob = cnt["of"] % 2
                    cnt["of"] += 1
                    for h in range(4):
                        pob = cnt["po"] % 3
                        cnt["po"] += 1
                        MMG([(po[pob][:, :], qb[b][:, h * 2, cl_ * CH:(cl_ + 1) * CH], Sbf[:, h * 2, :]),
                             (po[pob][:, :], qb[b][:, h * 2 + 1, cl_ * CH:(cl_ + 1) * CH], Sbf[:, h * 2 + 1, :]),
                             (po[pob][:, :], attm[ab][:, h, :], vb[b][:, cl_, h * 512:(h + 1) * 512])],
                            r=[r_qb[b], r_Sbf[h * 2], r_Sbf[h * 2 + 1], r_attm[ab], r_vb[b]], w=[r_po[pob]])
                        COPY("act", ofs[ob][:, h * 512:(h + 1) * 512], po[pob][:, :], r=[r_po[pob]], w=[r_ofs[ob]])
                        for dk in range(2):
                            zb = cnt["pz"] % 3
                            cnt["pz"] += 1
                            si = h * 2 + dk
                            MMG([(pz[zb][:, :], kdb[b][:, cl_, si * 128:(si + 1) * 128], vb[b][:, cl_, h * 512:(h + 1) * 512])],
                                r=[r_kdb[b], r_vb[b]], w=[r_pz[zb]])
                            STT("dve", Sst[:, si, :], Sst[:, si, :], dec[:, dr, si, gch:gch + 1], pz[zb][:, :],
                                ALU.mult, ALU.add, r=[r_S[si], r_dec, r_pz[zb]], w=[r_S[si]])
                            ce = "dve" if (cnt["cast"] % 4 == 3) else "act"
                            cnt["cast"] += 1
                            COPY(ce, Sbf[:, si, :], Sst[:, si, :], r=[r_S[si]], w=[r_Sbf[si]])
                    DMA("pool", o_d[dr, tk:tk + CH, :], ofs[ob][:, :], r=[r_ofs[ob]], w=[r_o[dr]])
                if kind == 1:
                    dst = (nsf_d if dr == 0 else nsb_d)[pidx].rearrange("h (c p) v -> p (h c) v", p=128)
                    DMA("pool", dst, Sst[:, :, :], r=r_S)
            P.run()

        with ExitStack() as st:
            P = Phase(S)
            cur[0] = P
            NBUF = 4
            gnw = sbt(st, "gnw", [128, 2048], F32)
            oa = [sbt(st, f"oa{i}", [128, 2048], F32) for i in range(NBUF)]
            obt = [sbt(st, f"obt{i}", [128, 2048], F32) for i in range(NBUF)]
            ggt = [sbt(st, f"ggt{i}", [128, 2048], BF16) for i in range(NBUF)]
            sgt = [sbt(st, f"sgt{i}", [128, 2048], F32) for i in range(NBUF)]
            gob = [sbt(st, f"gob{i}", [128, 2048], BF16) for i in range(NBUF)]
            gost = [sbt(st, f"gost{i}", [128, 16, 128], BF16) for i in range(NBUF)]
            nst = [sbt(st, f"nst{i}", [128, 16], F32) for i in range(NBUF)]
            junk = sbt(st, "junk", [128, 512], BF16)
            ptg = [[pst(st, f"ptg{i}_{j}", [128, 1024], BF16) for j in range(2)] for i in range(2)]
            r_gnw, r_junk = Res(), Res()
            r_oa, r_obt, r_ggt, r_sgt, r_gob, r_gost, r_nst = [[Res() for _ in range(NBUF)] for _ in range(7)]
            r_ptg = [[Res(), Res()], [Res(), Res()]]
            DMA("sp", gnw[:, :], gnw_d.partition_broadcast(128), w=[r_gnw])
            def p3b_loads(ti):
                tk = ti * 128
                b = ti % NBUF
                DMA("sp", oa[b][:, :], o_d[0, tk:tk + 128, :], r=[r_o[0]], w=[r_oa[b]])
                DMA("sp", obt[b][:, :], o_d[1, tk:tk + 128, :], r=[r_o[1]], w=[r_obt[b]])
                DMA("sp", ggt[b][:, :], gg_d[tk:tk + 128, :], r=[r_gg], w=[r_ggt[b]])

            NTI = NT // 128

            def stageA1(ti):
                b = ti % NBUF
                TT("dve", oa[b][:, :], oa[b][:, :], obt[b][:, :], ALU.add, r=[r_oa[b], r_obt[b]], w=[r_oa[b]])
                for h in range(4):
                    ACT(junk[:, :], oa[b][:, h * 512:(h + 1) * 512], AF.Square, r=[r_oa[b]], w=[r_junk, r_nst[b]],
                        accum_out=nst[b][:, h:h + 1])
                ACT(sgt[b][:, :], ggt[b][:, :], AF.Silu, r=[r_ggt[b]], w=[r_sgt[b]])
                TT("pool", sgt[b][:, :], sgt[b][:, :], gnw[:, :], ALU.mult, r=[r_sgt[b], r_gnw], w=[r_sgt[b]])

            def stageA2(ti):
                b = ti % NBUF
                TS("dve", nst[b][:, 4:8], nst[b][:, 0:4], 1.0 / 512, 1e-6, ALU.mult, ALU.add, r=[r_nst[b]], w=[r_nst[b]])
                cur[0].op("act", lambda e, b=b: e.sqrt(out=nst[b][:, 8:12], in_=nst[b][:, 4:8]), reads=[r_nst[b]], writes=[r_nst[b]])
                cur[0].op("dve", lambda e, b=b: e.reciprocal(out=nst[b][:, 12:16], in_=nst[b][:, 8:12]), reads=[r_nst[b]], writes=[r_nst[b]])

            def stageB(ti):
                tk = ti * 128
                b = ti % NBUF
                for h in range(4):
                    STT("dve", gob[b][:, h * 512:(h + 1) * 512], oa[b][:, h * 512:(h + 1) * 512],
                        nst[b][:, 12 + h:13 + h], sgt[b][:, h * 512:(h + 1) * 512], ALU.mult, ALU.mult,
                        r=[r_oa[b], r_nst[b], r_sgt[b]], w=[r_gob[b]])
                pb = ti % 2
                for hf in range(2):
                    TRS([(ptg[pb][hf][:, j * 128:(j + 1) * 128], gob[b][:, (hf * 8 + j) * 128:(hf * 8 + j + 1) * 128]) for j in range(8)],
                        idb[:, :], r=[r_gob[b], r_id], w=[r_ptg[pb][hf]])
                    COPY("dve", gost[b][:, hf * 8:(hf + 1) * 8, :].rearrange("p j n -> p (j n)"), ptg[pb][hf][:, :],
                         r=[r_ptg[pb][hf]], w=[r_gost[b]])
                DMA("pool", catT_d[2048:4096, tk:tk + 128].rearrange("(j p) n -> p j n", p=128), gost[b][:, :, :],
                    r=[r_gost[b]], w=[r_catT])

            p3b_loads(0)
            if NTI > 1:
                p3b_loads(1)
            stageA1(0)
            stageA2(0)
            for ti in range(NTI):
                if ti + 2 < NTI:
                    p3b_loads(ti + 2)
                if ti + 1 < NTI:
                    stageA1(ti + 1)
                stageB(ti)
                if ti + 1 < NTI:
                    stageA2(ti + 1)
            P.run()

        mid.close()
        with ExitStack() as st:
            P = Phase(S)
            cur[0] = P
            cT = sbt(st, "cT", [128, 32, 512], BF16)
            Wo = [sbt(st, f"Wo{i}", [128, 32, 512], BF16) for i in range(2)]
            yt = [sbt(st, f"yt{i}", [128, D], F32) for i in range(4)]
            gtb = sbt(st, "gtb", [128, D], F32)
            fnwb = sbt(st, "fnwb", [128, D], F32)
            tmp = [sbt(st, f"tmp{i}", [128, 512], F32) for i in range(2)]
            jk = sbt(st, "jk", [128, 512], BF16)
            ns8 = sbt(st, "ns8", [128, 4, 8], F32)
            ns4 = sbt(st, "ns4", [128, 4], F32)
            r_ns8 = [Res() for _ in range(4)]
            pm4 = [pst(st, f"pm4_{i}", [128, 512], F32) for i in range(4)]
            r_gtb, r_fnwb, r_jk, r_ns4 = [Res() for _ in range(4)]
            r_Wo, r_tmp = [Res(), Res()], [Res(), Res()]
            r_yt = [Res() for _ in range(4)]
            r_pm4 = [Res() for _ in range(4)]
            DMA("sp", fnwb[:, :], fnw_d.partition_broadcast(128), w=[r_fnwb])
            wl = [0]
            pi4 = [0]
            last_kind = -1

            def load_Wo(cb):
                b = wl[0] % 2
                wl[0] += 1
                DMA("sp", Wo[b].ap().rearrange("p c n -> p (c n)"), woutb_d[cb], r=[r_woutb[cb]], w=[r_Wo[b]])
                return b

            r_cTp = [Res() for _ in range(2)]

            def load_cT(bj, hf):
                t0_, TB_ = blocks[bj][0], blocks[bj][1]
                lo = hf * 256
                if lo >= TB_:
                    return
                DMA("sp", cT[:, :, lo:lo + 256],
                    catT_d[:, t0_ + lo:t0_ + lo + 256].rearrange("(c p) n -> p c n", p=128),
                    r=[r_catT], w=[r_cTp[hf]])

            def load_x(bj, tt):
                t0_ = blocks[bj][0]
                DMA("pool", yt[tt][:, :], x_d[t0_ + tt * 128:t0_ + (tt + 1) * 128, :], w=[r_yt[tt]])

            wb_next = load_Wo(0)
            load_cT(0, 0)
            load_cT(0, 1)
            for tt in range(blocks[0][1] // 128):
                load_x(0, tt)
            for bi, (t0, TB, kind) in enumerate(blocks):
                NTT = TB // 128
                if kind != last_kind:
                    DMA("sp", gtb[:, :], mod_d[kind, 2 * D:3 * D].partition_broadcast(128), r=[r_mod_d], w=[r_gtb])
                    last_kind = kind
                for cb in range(8):
                    wb = wb_next
                    if cb + 1 < 8:
                        wb_next = load_Wo(cb + 1)
                    elif bi + 1 < len(blocks):
                        wb_next = load_Wo(0)
                    for tt in range(NTT):
                        pb = pi4[0] % 4
                        pi4[0] += 1
                        tb_ = pi4[0] % 2
                        MMG([(pm4[pb][:, :], cT[:, kc, tt * 128:(tt + 1) * 128], Wo[wb][:, kc, :]) for kc in range(32)],
                            r=[r_cTp[tt // 2], r_Wo[wb]], w=[r_pm4[pb]])
                        TT("dve", tmp[tb_][:, :], pm4[pb][:, :], gtb[:, cb * 512:(cb + 1) * 512], ALU.mult,
                           r=[r_pm4[pb], r_gtb], w=[r_tmp[tb_]])
                        TT("pool", yt[tt][:, cb * 512:(cb + 1) * 512], yt[tt][:, cb * 512:(cb + 1) * 512], tmp[tb_][:, :], ALU.add,
                           r=[r_tmp[tb_], r_yt[tt]], w=[r_yt[tt]])
                        ACT(jk[:, :], yt[tt][:, cb * 512:(cb + 1) * 512], AF.Square, r=[r_yt[tt]], w=[r_jk, r_ns8[tt]],
                            accum_out=ns8[:, tt, cb:cb + 1])
                        if cb == 7:
                            cur[0].op("dve", lambda e, tt=tt: e.reduce_sum(out=ns4[:, 0:1], in_=ns8[:, tt, :], axis=mybir.AxisListType.X),
                                      reads=[r_ns8[tt]], writes=[r_ns4])
                            TS("dve", ns4[:, 1:2], ns4[:, 0:1], 1.0 / D, 1e-6, ALU.mult, ALU.add, r=[r_ns4], w=[r_ns4])
                            cur[0].op("act", lambda e: e.sqrt(out=ns4[:, 2:3], in_=ns4[:, 1:2]), reads=[r_ns4], writes=[r_ns4])
                            cur[0].op("dve", lambda e: e.reciprocal(out=ns4[:, 3:4], in_=ns4[:, 2:3]), reads=[r_ns4], writes=[r_ns4])
                            STT("dve", yt[tt][:, :], yt[tt][:, :], ns4[:, 3:4], fnwb[:, :], ALU.mult, ALU.mult,
                                r=[r_yt[tt], r_ns4, r_fnwb], w=[r_yt[tt]])
                            DMA("act", y_d[t0 + tt * 128:t0 + (tt + 1) * 128, :], yt[tt][:, :], r=[r_yt[tt]], w=[r_yt[tt]])
                            if bi + 1 < len(blocks) and tt % 2 == 1:
                                load_cT(bi + 1, tt // 2)
                    if cb == 7 and bi + 1 < len(blocks):
                        for tt in range(blocks[bi + 1][1] // 128):
                            load_x(bi + 1, tt)
            P.run()
    return nc


_CACHE = {}


def core_inputs(i, inp, consts, n_prompt_per_core):
    f = lambda a: np.ascontiguousarray(a, dtype=np.float32)
    npc = n_prompt_per_core
    xs = inp["x_sample"][i].reshape(-1, D)
    xp = inp["x_prompt"][i * npc:(i + 1) * npc].reshape(-1, D)
    d = {
        "x": f(np.concatenate([xs, xp], axis=0)),
        "s0f": f(inp["state_gla_fwd"][i, 0]),
        "s0b": f(inp["state_gla_bwd"][i, 0]),
        "c2": f(np.concatenate([inp["c"][i], inp["c_ctx"]])),
        "ada_w": f(inp["ada_w"][0]),
        "ada_b": f(inp["ada_b"][0]),
        "norm_w": f(inp["norm_w"][0]),
        "w_in": f(inp["w_in"][0]),
        "w_alpha": f(inp["w_alpha"][0]),
        "b_alpha": f(inp["b_alpha"][0].reshape(-1)),
        "w_fourier": f(inp["w_fourier"][0]),
        "gla_norm_w": f(inp["gla_norm_w"][0].reshape(-1)),
        "w_out": f(inp["w_out"][0]),
        "final_norm_w": f(inp["final_norm_w"]),
    }
    d.update(consts)
    return d


def kernel(**inputs):
    inp = {k: np.asarray(v) for k, v in inputs.items()}
    n = 8
    NB = inp["x_sample"].shape[0]
    NPB = inp["x_prompt"].shape[0]
    assert NB == n and NPB % n == 0
    npc = NPB // n
    NS = inp["x_sample"].shape[1]
    consts = make_consts()
    key = (NS, npc)
    if key not in _CACHE:
        _CACHE[key] = build(NS, npc)
    nc = _CACHE[key]
    in_maps = [core_inputs(i, inp, consts, npc) for i in range(n)]
    res = run_bass_kernel_spmd(nc, in_maps, core_ids=list(range(n)))
    outs = res.results
    y_s = np.stack([outs[i]["y"][:NS] for i in range(n)], axis=0).astype(np.float32)
    y_p = np.concatenate([outs[i]["y"][NS:].reshape(npc, 256, D) for i in range(n)], axis=0).astype(np.float32)
    nsf = np.concatenate([outs[i]["nsf"][:npc] for i in range(n)], axis=0)[:, None].astype(np.float32)
    nsb = np.concatenate([outs[i]["nsb"][:npc] for i in range(n)], axis=0)[:, None].astype(np.float32)
    return (y_p, y_s, nsf, nsb)
```

```python
import numpy as np
from contextlib import ExitStack
import concourse.bass as bass
import concourse.mybir as mybir
from concourse.bass_utils import run_bass_kernel_spmd

F32 = mybir.dt.float32
BF16 = mybir.dt.bfloat16
AF = mybir.ActivationFunctionType
ALU = mybir.AluOpType

ENGS = ("sp", "act", "pe", "dve", "pool")
NDMA = 8

D = 4096
NIN = 10272
NMOD = 3 * D


class Res:
    __slots__ = ("name", "lw", "rd")

    def __init__(self, name=""):
        self.name = name
        self.lw = None
        self.rd = []


class Op:
    __slots__ = ("eng", "fn", "deps", "signal", "sigval", "dma", "dsem", "dval", "dprev", "phase")

    def __init__(self, eng, fn, dma):
        self.eng = eng
        self.fn = fn
        self.deps = []
        self.signal = False
        self.sigval = None
        self.dma = dma
        self.dsem = None
        self.dval = None
        self.dprev = None


class Sync:
    def __init__(self, nc, es):
        self.nc = nc
        self.sem = {e: es.enter_context(nc.semaphore("s_" + e)) for e in ENGS}
        self.cnt = {e: 0 for e in ENGS}
        self.dq = ("sp", "act", "pool")
        self.dsem = {q: [es.enter_context(nc.semaphore(f"d_{q}{i}")) for i in range(NDMA)] for q in self.dq}
        self.dtot = {q: [0] * NDMA for q in self.dq}
        self.dnext = {q: 0 for q in self.dq}
        self.nops = 0

    def clear_all(self):
        nc = self.nc
        with nc.Block() as block:
            @block.gpsimd
            def _(e):
                for s in list(self.sem.values()) + [x for v in self.dsem.values() for x in v]:
                    e.sem_clear(s)


class Phase:
    def __init__(self, sync):
        self.S = sync
        self.ops = []
        self.q = {e: [] for e in ENGS}

    def op(self, eng, fn, reads=(), writes=(), dma=False, ndma=1):
        o = Op(eng, fn, dma)
        o.phase = self
        deps = set()
        for r in reads:
            if r.lw is not None:
                deps.add(r.lw)
        for w in writes:
            if w.lw is not None:
                deps.add(w.lw)
            for k in w.rd:
                deps.add(k)
        for d in deps:
            if d.phase is not self:
                continue
            if d.dma:
                o.deps.append(d)
            else:
                if d.eng == "pe" and eng == "pe":
                    continue
                d.signal = True
                o.deps.append(d)
        if dma:
            S = self.S
            i = S.dnext[eng]
            S.dnext[eng] = (i + 1) % NDMA
            o.dsem = (eng, i)
            o.dprev = S.dtot[eng][i]
            S.dtot[eng][i] += 16 * ndma
            o.dval = S.dtot[eng][i]
        for r in reads:
            r.rd.append(o)
        for w in writes:
            w.lw = o
            w.rd = []
        self.ops.append(o)
        self.q[eng].append(o)
        return o

    def run(self):
        S = self.S
        nc = S.nc
        for e in ENGS:
            for o in self.q[e]:
                if o.signal and not o.dma:
                    S.cnt[e] += 1
                    o.sigval = S.cnt[e]
        S.nops += len(self.ops)

        def replay(ename, eng):
            waited = {}

            def wait(sem, key, val):
                if waited.get(key, 0) >= val:
                    return
                waited[key] = val
                eng.wait_ge(sem, val)

            for o in self.q[ename]:
                for d in o.deps:
                    if d.dma:
                        q, i = d.dsem
                        wait(S.dsem[q][i], ("d", q, i), d.dval)
                    else:
                        wait(S.sem[d.eng], ("e", d.eng), d.sigval)
                if o.dma:
                    q, i = o.dsem
                    if o.dprev > 0:
                        wait(S.dsem[q][i], ("d", q, i), o.dprev)
                    ins = o.fn(eng)
                    if not isinstance(ins, (list, tuple)):
                        ins = [ins]
                    for x in ins:
                        x.then_inc(S.dsem[q][i], 16)
                else:
                    ins = o.fn(eng)
                    if isinstance(ins, (list, tuple)):
                        ins = ins[-1]
                    if o.signal:
                        ins.then_inc(S.sem[ename], 1)
            if ename in S.dsem:
                for i in range(NDMA):
                    if S.dtot[ename][i] > 0:
                        wait(S.dsem[ename][i], ("d", ename, i), S.dtot[ename][i])

        with nc.Block() as block:
            @block.sync
            def _(e):
                replay("sp", e)

            @block.scalar
            def _(e):
                replay("act", e)

            @block.tensor
            def _(e):
                replay("pe", e)

            @block.vector
            def _(e):
                replay("dve", e)

            @block.gpsimd
            def _(e):
                replay("pool", e)


def make_consts():
    def cs(n, scale):
        i = np.arange(n)
        ang = 2.0 * np.pi * ((i[:, None] * i[None, :]) % n) / n
        return np.cos(ang) * scale, np.sin(ang) * scale

    C64, S64 = cs(64, 1.0 / 8.0)
    Z = np.zeros((64, 64))
    BDc = np.block([[C64, Z], [Z, C64]])
    BDs = np.block([[S64, Z], [Z, S64]])
    bdR = np.concatenate([BDc, BDs], axis=1)
    bdW = np.concatenate([BDc, BDs, -BDs, BDc], axis=1)
    CL, SL = cs(256, 1.0 / 16.0)
    dftL = np.concatenate([CL, SL], axis=1)
    Cc, Sc = cs(512, 512 ** -0.5)
    dftc = np.concatenate([Cc, -Sc], axis=1)
    j = np.arange(128)
    maskf = (j[:, None] <= j[None, :]).astype(np.float32)
    maskb = (j[:, None] >= j[None, :]).astype(np.float32)
    masks = np.stack([maskf, maskb], axis=1)
    f = lambda a: np.ascontiguousarray(a, dtype=np.float32)
    return {"ident": f(np.eye(128)), "bdR": f(bdR), "bdW": f(bdW), "dftL": f(dftL), "dftc": f(dftc),
            "masks": f(masks)}


def build(n_samp=4096, n_prompt=4, debug=False):
    NS = n_samp
    NP = n_prompt
    NT = NS + 256 * NP
    nc = bass.Bass("TRN2", target_bir_lowering=False)
    dbg_kind = "ExternalOutput" if debug else "Internal"

    def din(name, shape):
        return nc.dram_tensor(name, list(shape), F32, kind="ExternalInput").ap()

    def dout(name, shape):
        return nc.dram_tensor(name, list(shape), F32, kind="ExternalOutput").ap()

    def dscr(name, shape, dt=BF16):
        return nc.dram_tensor(name, list(shape), dt, kind=dbg_kind).ap()

    x_d = din("x", [NT, D])
    s0f_d = din("s0f", [4, 256, 512])
    s0b_d = din("s0b", [4, 256, 512])
    c2_d = din("c2", [2 * D])
    adaw_d = din("ada_w", [D, NMOD])
    adab_d = din("ada_b", [NMOD])
    normw_d = din("norm_w", [D])
    win_d = din("w_in", [D, NIN])
    walpha_d = din("w_alpha", [2, 16, 1024])
    balpha_d = din("b_alpha", [2 * 1024])
    wf_d = din("w_fourier", [4, 512, 512])
    gnw_d = din("gla_norm_w", [2048])
    wout_d = din("w_out", [D, D])
    fnw_d = din("final_norm_w", [D])
    ident_d = din("ident", [128, 128])
    bdR_d = din("bdR", [128, 256])
    bdW_d = din("bdW", [128, 512])
    dftL_d = din("dftL", [256, 512])
    dftc_d = din("dftc", [512, 1024])
    masks_d = din("masks", [128, 2, 128])

    y_d = dout("y", [NT, D])
    nsf_d = dout("nsf", [max(NP, 1), 4, 256, 512])
    nsb_d = dout("nsb", [max(NP, 1), 4, 256, 512])

    mod_d = dscr("mod_d", [2, NMOD], F32)
    winb_d = dscr("winb", [20, 128, 32 * 512])
    woutb_d = dscr("woutb", [8, 128, 32 * 512])
    mgb_d = dscr("mgb", [4, 128, 8 * 512])
    fin_d = dscr("fin", [NT, 2048])
    fgT_d = dscr("fgT", [2048, NT])
    qtT_d = dscr("qtT", [2, 1024, NT])
    ktT_d = dscr("ktT", [2, 1024, NT])
    kd_d = dscr("kd", [2, NT, 1024])
    v_d = dscr("v", [NT, 2048])
    gg_d = dscr("gg", [NT, 2048])
    pq1_d = dscr("pq1", [2, max(NS, 128), 2048])
    catT_d = dscr("catT", [D, NT])
    o_d = dscr("o_dir", [2, NT, 2048], F32)

    blocks = []
    for b in range(NS // 512):
        blocks.append((b * 512, 512, 0))
    t = NS
    rem = 256 * NP
    while rem > 0:
        n = min(512, rem)
        blocks.append((t, n, 1))
        t += n
        rem -= n
    seqs = []
    if NS:
        seqs.append((0, NS, 0, -1))
    for p in range(NP):
        seqs.append((NS + 256 * p, 256, 1, p))
    CH = 128
    NCH = NT // CH

    with ExitStack() as es:
        S = Sync(nc, es)
        S.clear_all()
        cur = [None]

        def sbt(st, name, shape, dt):
            return st.enter_context(nc.sbuf_tensor(name, list(shape), dt))

        def pst(st, name, shape, dt):
            return st.enter_context(nc.psum_tensor(name, list(shape), dt))

        def DMA(q, out, in_, r=(), w=()):
            cur[0].op(q, lambda e: e.dma_start(out=out, in_=in_), reads=r, writes=w, dma=True)

        def ACT(out, in_, func, r=(), w=(), **kw):
            cur[0].op("act", lambda e: e.activation(out=out, in_=in_, func=func, **kw), reads=r, writes=w)

        def TT(eng, out, a, b, op, r=(), w=()):
            cur[0].op(eng, lambda e: e.tensor_tensor(out=out, in0=a, in1=b, op=op), reads=r, writes=w)

        def STT(eng, out, in0, scalar, in1, op0, op1, r=(), w=()):
            cur[0].op(eng, lambda e: e.scalar_tensor_tensor(out=out, in0=in0, scalar=scalar, in1=in1, op0=op0, op1=op1),
                      reads=r, writes=w)

        def TS(eng, out, in0, s1, s2, op0, op1, r=(), w=()):
            cur[0].op(eng, lambda e: e.tensor_scalar(out=out, in0=in0, scalar1=s1, scalar2=s2, op0=op0, op1=op1),
                      reads=r, writes=w)

        def COPY(eng, out, in_, r=(), w=()):
            if eng == "act":
                cur[0].op("act", lambda e: e.activation(out=out, in_=in_, func=AF.Copy), reads=r, writes=w)
            else:
                cur[0].op(eng, lambda e: e.tensor_copy(out=out, in_=in_), reads=r, writes=w)

        def MEMSET(eng, ap, val, r=(), w=()):
            cur[0].op(eng, lambda e: e.memset(ap, val), reads=r, writes=w)

        def MMG(items, r=(), w=()):
            def f(e):
                n = len(items)
                for i, (o, l, rr) in enumerate(items):
                    ins = e.matmul(o, lhsT=l, rhs=rr, start=(i == 0), stop=(i == n - 1))
                return ins
            cur[0].op("pe", f, reads=r, writes=w)

        def TRS(items, ident, r=(), w=()):
            def f(e):
                for (o, i_) in items:
                    ins = e.transpose(out=o, in_=i_, identity=ident)
                return ins
            cur[0].op("pe", f, reads=r, writes=w)

        idf = sbt(es, "idf", [128, 128], F32)
        idb = sbt(es, "idb", [128, 128], BF16)
        ABt = sbt(es, "ABt", [128, 2, 2, 32], F32)
        modT = sbt(es, "modT", [128, 2, 96], F32)
        nbaT = sbt(es, "nbaT", [128, 16], F32)
        mid = ExitStack()
        walb = sbt(mid, "walb", [16, 2, 1024], BF16)
        wab = sbt(mid, "wab", [128, 32, 32], BF16)
        dec = sbt(mid, "dec", [128, 2, 8, NCH], F32)
        maskb16 = sbt(mid, "maskb16", [128, 2, 128], BF16)
        scanm = sbt(mid, "scanm", [128, 512], F32)
        r_id, r_AB, r_modT, r_nba, r_wal, r_wab, r_dec, r_mask, r_scanm = [Res() for _ in range(9)]
        r_mod_d, r_winb, r_woutb, r_mgb = Res(), [Res() for _ in range(20)], [Res() for _ in range(8)], [Res() for _ in range(4)]

        with ExitStack() as st:
            P = Phase(S)
            cur[0] = P
            c2t = sbt(st, "c2t", [64, 128], F32)
            sct = sbt(st, "sct", [64, 128], F32)
            scT2 = sbt(st, "scT2", [128, 32, 2], BF16)
            adab2 = [sbt(st, f"adab2_{i}", [2, 512], F32) for i in range(2)]
            modrow = [sbt(st, f"modrow{i}", [2, 512], F32) for i in range(2)]
            adaw = [sbt(st, f"adaw{i}", [128, 16, 512], F32) for i in range(3)]
            adawb = [sbt(st, f"adawb{i}", [128, 32, 512], BF16) for i in range(2)]
            r_adawb = [[Res(), Res()], [Res(), Res()]]
            mk_f = sbt(st, "mk_f", [128, 2, 128], F32)
            nwt = sbt(st, "nwt", [32, 128], F32)
            bat = sbt(st, "bat", [16, 128], F32)
            walf = sbt(st, "walf", [16, 2, 1024], F32)
            waf = sbt(st, "waf", [128, 32, 32], F32)
            m96 = [sbt(st, f"m96_{k}", [96, 128], F32) for k in range(2)]
            normT = sbt(st, "normT", [128, 32], F32)
            pmod = [pst(st, f"pmod{i}", [128, 512], F32) for i in range(2)]
            ptr = pst(st, "ptr0a", [128, 512], F32)
            r_c2, r_sct, r_scT2 = Res(), Res(), Res()
            r_adab2, r_modrow = [Res(), Res()], [Res(), Res()]
            r_adaw = [Res(), Res(), Res()]
            r_pmod = [Res(), Res()]
            r_ptr = Res()
            r_misc = Res()

            DMA("sp", idf[:, :], ident_d, w=[r_id])
            COPY("dve", idb[:, :], idf[:, :], r=[r_id], w=[r_id])
            DMA("sp", c2t[:, :], c2_d.rearrange("(a b) -> a b", b=128), w=[r_c2])
            ACT(sct[:, :], c2t[:, :], AF.Silu, r=[r_c2], w=[r_sct])
            TRS([(ptr[:, 0:64], sct[:, :])], idf[0:64, 0:64], r=[r_sct, r_id], w=[r_ptr])
            COPY("dve", scT2.ap().rearrange("p k m -> p m k"), ptr[:, 0:64].rearrange("p (m k) -> p m k", m=2),
                 r=[r_ptr], w=[r_scT2])
            adaw_v = adaw_d.rearrange("(c p) n -> p c n", p=128)
            NCB = NMOD // 512
            ak = 0
            for cb in range(NCB):
                b = cb % 2
                for hh in range(2):
                    fb = ak % 3
                    DMA("sp" if ak % 2 == 0 else "pool", adaw[fb][:, :, :], adaw_v[:, hh * 16:(hh + 1) * 16, cb * 512:(cb + 1) * 512], w=[r_adaw[fb]])
                    COPY("act" if ak % 2 else "dve", adawb[b][:, hh * 16:(hh + 1) * 16, :], adaw[fb][:, :, :],
                         r=[r_adaw[fb]], w=[r_adawb[b][hh]])
                    ak += 1
                DMA("sp", adab2[b][:, :], adab_d[cb * 512:(cb + 1) * 512].partition_broadcast(2), w=[r_adab2[b]])
                MMG([(pmod[b][0:2, :], scT2[:, kc, :], adawb[b][:, kc, :]) for kc in range(32)],
                    r=[r_scT2] + r_adawb[b], w=[r_pmod[b]])
                TT("dve", modrow[b][:, :], pmod[b][0:2, :], adab2[b][:, :], ALU.add,
                   r=[r_pmod[b], r_adab2[b]], w=[r_modrow[b]])
                DMA("act", mod_d[:, cb * 512:(cb + 1) * 512], modrow[b][:, :], r=[r_modrow[b]], w=[r_mod_d])
            for k in range(2):
                DMA("sp", m96[k][:, :], mod_d[k].rearrange("(a b) -> a b", b=128), r=[r_mod_d], w=[r_misc])
                TRS([(ptr[:, 0:96], m96[k][:, :])], idf[0:96, 0:96], r=[r_misc, r_id], w=[r_ptr])
                COPY("dve", modT[:, k, :], ptr[:, 0:96], r=[r_ptr], w=[r_modT])
            DMA("sp", nwt[:, :], normw_d.rearrange("(a b) -> a b", b=128), w=[r_misc])
            TRS([(ptr[:, 0:32], nwt[:, :])], idf[0:32, 0:32], r=[r_misc, r_id], w=[r_ptr])
            COPY("dve", normT[:, :], ptr[:, 0:32], r=[r_ptr], w=[r_misc])
            for k in range(2):
                STT("dve", ABt[:, k, 0, :], modT[:, k, 32:64], 1.0, normT[:, :], ALU.add, ALU.mult,
                    r=[r_modT, r_misc], w=[r_AB])
                COPY("dve", ABt[:, k, 1, :], modT[:, k, 0:32], r=[r_modT], w=[r_AB])
            DMA("sp", bat[:, :], balpha_d.rearrange("(a b) -> a b", b=128), w=[r_misc])
            TRS([(ptr[:, 0:16], bat[:, :])], idf[0:16, 0:16], r=[r_misc, r_id], w=[r_ptr])
            TS("dve", nbaT[:, :], ptr[:, 0:16], -1.0, 0.0, ALU.mult, ALU.add, r=[r_ptr], w=[r_nba])
            DMA("sp", walf[:, :, :], walpha_d.rearrange("d k n -> k d n"), w=[r_misc])
            COPY("dve", walb[:, :, :], walf[:, :, :], r=[r_misc], w=[r_wal])
            DMA("sp", waf[:, :, :], win_d.rearrange("(c p) n -> p c n", p=128)[:, :, 10240:10272], w=[r_misc])
            COPY("dve", wab[:, :, :], waf[:, :, :], r=[r_misc], w=[r_wab])
            DMA("sp", mk_f[:, :, :], masks_d, w=[r_misc])
            COPY("dve", maskb16[:, :, :], mk_f[:, :, :], r=[r_misc], w=[r_mask])
            MEMSET("pool", scanm[:, :], 1.0, w=[r_scanm])
            MEMSET("pool", scanm.ap().rearrange("p (c t) -> p c t", t=CH)[:, :, 0:1], 0.0, w=[r_scanm])
            P.run()

        with ExitStack() as st:
            P = Phase(S)
            cur[0] = P
            NB = 4
            wf32 = [sbt(st, f"wf32_{i}", [128, 16, 512], F32) for i in range(NB)]
            wb16 = [sbt(st, f"wb16_{i}", [128, 16 * 512], BF16) for i in range(NB)]
            r_wf32 = [Res() for _ in range(NB)]
            r_wb16 = [Res() for _ in range(NB)]
            k = 0
            cast_engs = ("dve", "act")
            for (src, dst, rdst, ncb) in ((win_d, winb_d, r_winb, 20),):
                sv = src.rearrange("(c p) n -> p c n", p=128)
                for cb in range(ncb):
                    for hh in range(2):
                        b = k % NB
                        DMA("sp" if k % 2 == 0 else "pool", wf32[b][:, :, :], sv[:, hh * 16:(hh + 1) * 16, cb * 512:(cb + 1) * 512],
                            w=[r_wf32[b]])
                        COPY(cast_engs[k % 2], wb16[b].ap().rearrange("p (c n) -> p c n", n=512), wf32[b][:, :, :],
                             r=[r_wf32[b]], w=[r_wb16[b]])
                        DMA("act", dst[cb][:, hh * 8192:(hh + 1) * 8192], wb16[b][:, :], r=[r_wb16[b]], w=[rdst[cb]])
                        k += 1
            P.run()
        with ExitStack() as st:
            P = Phase(S)
            cur[0] = P
            dcf = sbt(st, "dcf", [128, 4, 1024], F32)
            dcb = sbt(st, "dcb", [128, 4, 1024], BF16)
            wff = sbt(st, "wff", [128, 4, 512], F32)
            wfb = [sbt(st, f"wfb{i}", [128, 4, 512], BF16) for i in range(2)]
            mgt = [sbt(st, f"mgt{i}", [128, 8, 512], BF16) for i in range(2)]
            pmg = [pst(st, f"pmg{i}", [128, 512], F32) for i in range(2)]
            r_dc, r_wff, r_wfb, r_mgt, r_pmg = Res(), Res(), [Res(), Res()], [Res(), Res()], [Res(), Res()]
            DMA("sp", dcf[:, :, :], dftc_d.rearrange("(c p) n -> p c n", p=128), w=[r_dc])
            COPY("dve", dcb[:, :, :], dcf[:, :, :], r=[r_dc], w=[r_dc])
            kk = 0
            for g in range(4):
                DMA("sp", wff[:, :, :], wf_d[g].rearrange("(c p) n -> p c n", p=128), w=[r_wff])
                COPY("dve", wfb[g % 2][:, :, :], wff[:, :, :], r=[r_wff], w=[r_wfb[g % 2]])
                for kc in range(8):
                    pb = kk % 2
                    kk += 1
                    MMG([(pmg[pb][:, :], dcb[:, cc, kc * 128:(kc + 1) * 128], wfb[g % 2][:, cc, :]) for cc in range(4)],
                        r=[r_dc, r_wfb[g % 2]], w=[r_pmg[pb]])
                    COPY("act" if kc % 2 else "dve", mgt[g % 2][:, kc, :], pmg[pb][:, :], r=[r_pmg[pb]], w=[r_mgt[g % 2]])
                DMA("sp", mgb_d[g], mgt[g % 2].ap().rearrange("p k n -> p (k n)"), r=[r_mgt[g % 2]], w=[r_mgb[g]])
            P.run()

        r_fin, r_fgT, r_v, r_gg, r_catT, r_pq1 = [Res() for _ in range(6)]
        r_qtT, r_ktT, r_kd = [Res(), Res()], [Res(), Res()], [Res(), Res()]

        with ExitStack() as st:
            P = Phase(S)
            cur[0] = P
            hT = sbt(st, "hT", [128, 32, 512], BF16)
            Wt = [sbt(st, f"Wt{i}", [128, 32, 512], BF16) for i in range(2)]
            xts = [sbt(st, f"xt{i}", [128, D], F32) for i in range(2)]
            r_xts = [Res(), Res()]
            xti = [0]
            xn = sbt(st, "xn", [128, D], BF16)
            qT = sbt(st, "qT", [128, 8, 512], BF16)
            kT = sbt(st, "kT", [128, 8, 512], BF16)
            kdT = sbt(st, "kdT", [128, 8, 512], BF16)
            kdtm = sbt(st, "kdtm", [128, 4, 1024], BF16)
            stg = [sbt(st, f"stg{i}", [128, 4, 512], BF16) for i in range(2)]
            aT = sbt(st, "aT", [16, 2, 512], BF16)
            sst = sbt(st, "sst", [128, 4], F32)
            gl = [sbt(st, f"gl{i}", [128, 512], F32) for i in range(1)]
            gc = [sbt(st, f"gc{i}", [128, 512], F32) for i in range(1)]
            gd = [sbt(st, f"gd{i}", [128, 512], F32) for i in range(1)]
            ge = [sbt(st, f"ge{i}", [128, 512], F32) for i in range(3)]
            sq = [sbt(st, f"sq{i}", [128, 512], BF16) for i in range(2)]
            sk = [sbt(st, f"sk{i}", [128, 512], BF16) for i in range(2)]
            pT = [pst(st, f"pT{i}", [128, 1024], BF16) for i in range(2)]
            pm = [pst(st, f"pm{i}", [128, 512], F32) for i in range(4)]
            pg = [pst(st, f"pg{i}", [128, 512], F32) for i in range(2)]
            r_hT, r_xn, r_qT, r_kT, r_kdT, r_kdtm, r_aT, r_ss = [Res() for _ in range(8)]
            r_Wt = [Res(), Res()]
            r_stg = [Res(), Res()]
            r_gl, r_gc, r_gd = [Res(), Res()], [Res(), Res()], [Res(), Res()]
            r_ge = [Res() for _ in range(3)]
            r_sq, r_sk = [Res(), Res()], [Res(), Res()]
            r_pT, r_pm, r_pg = [Res(), Res()], [Res() for _ in range(4)], [Res(), Res()]
            wload = [0]
            pmi = [0]
            gi = [0]
            evi = [0]

            def load_W(cb):
                b = wload[0] % 2
                wload[0] += 1
                DMA("sp", Wt[b].ap().rearrange("p c n -> p (c n)"), winb_d[cb], r=[r_winb[cb]], w=[r_Wt[b]])
                return b

            nblk = len(blocks)
            wb_next = load_W(8)
            for bi, (t0, TB, kind) in enumerate(blocks):
                NTT = TB // 128
                nchb = TB // CH
                ch0 = t0 // CH
                for tt in range(NTT):
                    xt = xts[xti[0] % 2]
                    r_xt = r_xts[xti[0] % 2]
                    xti[0] += 1
                    DMA("pool", xt[:, :], x_d[t0 + tt * 128:t0 + (tt + 1) * 128, :], w=[r_xt])
                    ACT(xn[:, :], xt[:, :], AF.Square, r=[r_xt], w=[r_xn, r_ss], accum_out=sst[:, 0:1])
                    TS("dve", sst[:, 1:2], sst[:, 0:1], 1.0 / D, 1e-6, ALU.mult, ALU.add, r=[r_ss], w=[r_ss])
                    cur[0].op("act", lambda e: e.sqrt(out=sst[:, 2:3], in_=sst[:, 1:2]), reads=[r_ss], writes=[r_ss])
                    cur[0].op("dve", lambda e: e.reciprocal(out=sst[:, 3:4], in_=sst[:, 2:3]), reads=[r_ss], writes=[r_ss])
                    TS("dve", xn[:, 0:2048], xt[:, 0:2048], sst[:, 3:4], 0.0, ALU.mult, ALU.add, r=[r_xt, r_ss], w=[r_xn])
                    ACT(xn[:, 2048:4096], xt[:, 2048:4096], AF.Copy, r=[r_xt, r_ss], w=[r_xn], scale=sst[:, 3:4])
                    for g8 in range(4):
                        pb = g8 % 2
                        TRS([(pT[pb][:, i * 128:(i + 1) * 128], xn[:, (g8 * 8 + i) * 128:(g8 * 8 + i + 1) * 128]) for i in range(8)],
                            idb[:, :], r=[r_xn, r_id], w=[r_pT[pb]])

                        def f_ev(e, g8=g8, pb=pb, tt=tt, kind=kind, act=(g8 % 2 == 0)):
                            for i in range(8):
                                c = g8 * 8 + i
                                if act:
                                    ins = e.activation(out=hT[:, c, tt * 128:(tt + 1) * 128], in_=pT[pb][:, i * 128:(i + 1) * 128],
                                                       func=AF.Identity, scale=ABt[:, kind, 0, c:c + 1], bias=ABt[:, kind, 1, c:c + 1])
                                else:
                                    ins = e.tensor_scalar(out=hT[:, c, tt * 128:(tt + 1) * 128], in0=pT[pb][:, i * 128:(i + 1) * 128],
                                                          scalar1=ABt[:, kind, 0, c:c + 1], scalar2=ABt[:, kind, 1, c:c + 1],
                                                          op0=ALU.mult, op1=ALU.add)
                            return ins
                        cur[0].op("act" if g8 % 2 == 0 else "dve", f_ev, reads=[r_pT[pb], r_AB], writes=[r_hT])

                units = []

                def unit_a(dr, TB=TB):
                    pb = gi[0] % 2
                    gi[0] += 1
                    MMG([(pg[pb][0:16, 0:TB], wab[:, kc, dr * 16:(dr + 1) * 16], hT[:, kc, 0:TB]) for kc in range(32)],
                        r=[r_hT, r_wab], w=[r_pg[pb]])
                    COPY("dve", aT[:, dr, 0:TB], pg[pb][0:16, 0:TB], r=[r_pg[pb]], w=[r_aT])

                def unit_g(dr, c8, TB=TB, nchb=nchb, ch0=ch0, t0=t0):
                    pb = gi[0] % 2
                    gi[0] += 1
                    i2 = 0
                    i3 = c8 % 2
                    MMG([(pg[pb][:, 0:TB], walb[:, dr, c8 * 128:(c8 + 1) * 128], aT[:, dr, 0:TB])],
                        r=[r_wal, r_aT], w=[r_pg[pb]])
                    ACT(gl[i2][:, 0:TB], pg[pb][:, 0:TB], AF.Exp, r=[r_pg[pb], r_nba], w=[r_gl[i2]],
                        scale=-1.0, bias=nbaT[:, dr * 8 + c8:dr * 8 + c8 + 1])
                    ACT(gl[i2][:, 0:TB], gl[i2][:, 0:TB], AF.Ln, r=[r_gl[i2]], w=[r_gl[i2]], bias=1.0)
                    cur[0].op("dve", lambda e, i2=i2, TB=TB: e.tensor_tensor_scan(
                        out=gc[i2][:, 0:TB], data0=scanm[:, 0:TB], data1=gl[i2][:, 0:TB], initial=0.0,
                        op0=ALU.mult, op1=ALU.add), reads=[r_gl[i2], r_scanm], writes=[r_gc[i2]])
                    clv = gc[i2][:, 0:TB].rearrange("p (c t) -> p c t", t=CH)
                    dv_ = gd[i2][:, 0:TB].rearrange("p (c t) -> p c t", t=CH)
                    totb = clv[:, :, CH - 1:CH].to_broadcast([128, nchb, CH])
                    TT("dve", dv_, totb, clv, ALU.subtract, r=[r_gc[i2]], w=[r_gd[i2]])
                    if dr == 0:
                        dq_ap, dq_r = gc[i2][:, 0:TB], r_gc[i2]
                        dkd_ap, dkd_r = gd[i2][:, 0:TB], r_gd[i2]
                    else:
                        TT("pool", gd[i2][:, 0:TB], gd[i2][:, 0:TB], gl[i2][:, 0:TB], ALU.add,
                           r=[r_gd[i2], r_gl[i2]], w=[r_gd[i2]])
                        TT("pool", gl[i2][:, 0:TB], gc[i2][:, 0:TB], gl[i2][:, 0:TB], ALU.subtract,
                           r=[r_gc[i2], r_gl[i2]], w=[r_gl[i2]])
                        dq_ap, dq_r = gd[i2][:, 0:TB], r_gd[i2]
                        dkd_ap, dkd_r = gl[i2][:, 0:TB], r_gl[i2]
                    ACT(dec[:, dr, c8, ch0:ch0 + nchb], gc[i2][:, 0:TB].rearrange("p (c t) -> p c t", t=CH)[:, :, CH - 1],
                        AF.Exp, r=[r_gc[i2]], w=[r_dec], scale=-1.0 / 16.0)
                    ACT(ge[0][:, 0:TB], dq_ap, AF.Exp, r=[dq_r], w=[r_ge[0]], scale=-1.0 / 16.0)
                    TT("dve", sq[i3][:, 0:TB], qT[:, c8, 0:TB], ge[0][:, 0:TB], ALU.mult, r=[r_qT, r_ge[0]], w=[r_sq[i3]])
                    DMA("sp", qtT_d[dr, c8 * 128:(c8 + 1) * 128, t0:t0 + TB], sq[i3][:, 0:TB], r=[r_sq[i3]], w=[r_qtT[dr]])
                    ACT(ge[1][:, 0:TB], dq_ap, AF.Exp, r=[dq_r], w=[r_ge[1]], scale=1.0 / 16.0)
                    TT("pool", sk[i3][:, 0:TB], kT[:, c8, 0:TB], ge[1][:, 0:TB], ALU.mult, r=[r_kT, r_ge[1]], w=[r_sk[i3]])
                    DMA("sp", ktT_d[dr, c8 * 128:(c8 + 1) * 128, t0:t0 + TB], sk[i3][:, 0:TB], r=[r_sk[i3]], w=[r_ktT[dr]])
                    ACT(ge[2][:, 0:TB], dkd_ap, AF.Exp, r=[dkd_r], w=[r_ge[2]], scale=-1.0 / 16.0)
                    TT("dve", kdT[:, c8, 0:TB], kT[:, c8, 0:TB], ge[2][:, 0:TB], ALU.mult, r=[r_kT, r_ge[2]], w=[r_kdT])

                def unit_kd(dr, TB=TB, NTT=NTT, t0=t0):
                    for tt in range(NTT):
                        pb = tt % 2
                        TRS([(pT[pb][:, c8 * 128:(c8 + 1) * 128], kdT[:, c8, tt * 128:(tt + 1) * 128]) for c8 in range(8)],
                            idb[:, :], r=[r_kdT, r_id], w=[r_pT[pb]])
                        COPY("act" if tt % 2 else "dve", kdtm[:, tt, :], pT[pb][:, :], r=[r_pT[pb]], w=[r_kdtm])
                    DMA("sp", kd_d[dr, t0:t0 + TB, :].rearrange("(t p) n -> p t n", p=128), kdtm[:, 0:NTT, :],
                        r=[r_kdtm], w=[r_kd[dr]])

                for dr in range(2):
                    for c8 in range(8):
                        units.append(lambda dr=dr, c8=c8: unit_g(dr, c8))
                    units.append(lambda dr=dr: unit_kd(dr))

                cb_order = [8, 9, 10, 11] + [0, 1, 2, 3, 4, 5, 6, 7, 12, 13, 14, 15, 16, 17, 18, 19]
                for ci_, cb in enumerate(cb_order):
                    wb = wb_next
                    if ci_ + 1 < 20:
                        wb_next = load_W(cb_order[ci_ + 1])
                    elif bi + 1 < nblk:
                        wb_next = load_W(cb_order[0])
                    tokmajor = cb < 4 or cb >= 12
                    sb_ = evi[0] % 2
                    isqk = cb in (8, 9, 10, 11)
                    if not isqk:
                        evi[0] += 1
                    n_it = NTT if tokmajor else 4
                    for it in range(n_it):
                        pb = pmi[0] % 4
                        pmi[0] += 1
                        if tokmajor:
                            MMG([(pm[pb][:, :], hT[:, kc, it * 128:(it + 1) * 128], Wt[wb][:, kc, :]) for kc in range(32)],
                                r=[r_hT, r_Wt[wb]], w=[r_pm[pb]])
                        else:
                            MMG([(pm[pb][:, 0:TB], Wt[wb][:, kc, it * 128:(it + 1) * 128], hT[:, kc, 0:TB]) for kc in range(32)],
                                r=[r_hT, r_Wt[wb]], w=[r_pm[pb]])
                        ev = "act" if (pmi[0] % 2) else "dve"
                        if cb in (8, 9):
                            c8 = (cb - 8) * 4 + it
                            if ev == "act":
                                ACT(qT[:, c8, 0:TB], pm[pb][:, 0:TB], AF.Copy, r=[r_pm[pb]], w=[r_qT], scale=1.0 / 16.0)
                            else:
                                TS("dve", qT[:, c8, 0:TB], pm[pb][:, 0:TB], 1.0 / 16.0, 0.0, ALU.mult, ALU.add,
                                   r=[r_pm[pb]], w=[r_qT])
                        elif cb in (10, 11):
                            c8 = (cb - 10) * 4 + it
                            COPY(ev, kT[:, c8, 0:TB], pm[pb][:, 0:TB], r=[r_pm[pb]], w=[r_kT])
                        elif tokmajor:
                            COPY(ev, stg[sb_][:, it, :], pm[pb][:, :], r=[r_pm[pb]], w=[r_stg[sb_]])
                        else:
                            COPY(ev, stg[sb_][:, it, 0:TB], pm[pb][:, 0:TB], r=[r_pm[pb]], w=[r_stg[sb_]])
                    if cb < 4:
                        DMA("sp", fin_d[t0:t0 + TB, cb * 512:(cb + 1) * 512].rearrange("(t p) n -> p t n", p=128),
                            stg[sb_][:, 0:NTT, :], r=[r_stg[sb_]], w=[r_fin])
                    elif cb < 8:
                        DMA("sp", fgT_d[(cb - 4) * 512:(cb - 3) * 512, t0:t0 + TB].rearrange("(j p) n -> p j n", p=128),
                            stg[sb_][:, :, 0:TB], r=[r_stg[sb_]], w=[r_fgT])
                    elif cb >= 16:
                        DMA("sp", gg_d[t0:t0 + TB, (cb - 16) * 512:(cb - 15) * 512].rearrange("(t p) n -> p t n", p=128),
                            stg[sb_][:, 0:NTT, :], r=[r_stg[sb_]], w=[r_gg])
                    elif cb >= 12:
                        DMA("sp", v_d[t0:t0 + TB, (cb - 12) * 512:(cb - 11) * 512].rearrange("(t p) n -> p t n", p=128),
                            stg[sb_][:, 0:NTT, :], r=[r_stg[sb_]], w=[r_v])
                    if ci_ == 3:
                        unit_a(0)
                        unit_a(1)
                    elif ci_ > 3:
                        nu = 2 if (ci_ - 4) < 2 else 1
                        for _ in range(nu):
                            if units:
                                units.pop(0)()
                while units:
                    units.pop(0)()
            P.run()

        with ExitStack() as st:
            P = Phase(S)
            cur[0] = P
            mg = sbt(st, "mg", [128, 4, 8 * 512], BF16)
            cf = sbt(st, "cf", [128, 1280], F32)
            bdRb = sbt(st, "bdRb", [128, 256], BF16)
            bdWb = sbt(st, "bdWb", [128, 512], BF16)
            dLb = sbt(st, "dLb", [128, 2, 512], BF16)
            uin = [sbt(st, f"uin{i}", [128, 2048], BF16) for i in range(2)]
            pqs = [sbt(st, f"pqs{i}", [128, 2, 2048], BF16) for i in range(2)]
            pq1t = sbt(st, "pq1t", [128, 4, 2, 2048], BF16)
            PQT = sbt(st, "PQT", [128, 2, 16, 512], BF16)
            sfg = sbt(st, "sfg", [128, 16, 512], BF16)
            cst = [sbt(st, f"cst{i}", [128, 4, 512], BF16) for i in range(2)]
            pp = [pst(st, f"pp{i}", [128, 512], F32) for i in range(8)]
            r_mg, r_cf, r_bd, r_PQT, r_sfg, r_pq1t = [Res() for _ in range(6)]
            r_uin, r_pqs, r_cst = [Res(), Res(), Res()], [Res(), Res(), Res()], [Res(), Res()]
            r_pp = [Res() for _ in range(8)]
            ppi = [0]
            wof = [sbt(st, f"wof{i}", [128, 8, 512], F32) for i in range(2)]
            wob = [sbt(st, "wob0", [128, 8 * 512], BF16)] * 2
            r_wof = [Res(), Res()]
            _rw = Res()
            r_wob = [_rw, _rw]
            wo_v = wout_d.rearrange("(c p) n -> p c n", p=128)
            wo_jobs = [(cb, hh) for cb in range(8) for hh in range(4)]
            wo_pend = []
            wo_k = [0]

            def wo_job():
                if wo_pend:
                    cb, hh, b = wo_pend.pop(0)
                    COPY("act" if hh % 2 else "dve", wob[b].ap().rearrange("p (c n) -> p c n", n=512), wof[b][:, :, :],
                         r=[r_wof[b]], w=[r_wob[b]])
                    DMA("pool", woutb_d[cb][:, hh * 4096:(hh + 1) * 4096], wob[b][:, :], r=[r_wob[b]], w=[r_woutb[cb]])
                if wo_jobs:
                    cb, hh = wo_jobs.pop(0)
                    b = wo_k[0] % 2
                    wo_k[0] += 1
                    DMA("sp", wof[b][:, :, :], wo_v[:, hh * 8:(hh + 1) * 8, cb * 512:(cb + 1) * 512], w=[r_wof[b]])
                    wo_pend.append((cb, hh, b))

            def nextpp():
                i = ppi[0] % 8
                ppi[0] += 1
                return i

            for g in range(4):
                DMA("sp", mg[:, g, :], mgb_d[g], r=[r_mgb[g]], w=[r_mg])
            DMA("sp", cf[:, 0:256], bdR_d, w=[r_cf])
            DMA("sp", cf[:, 256:768], bdW_d, w=[r_cf])
            COPY("dve", bdRb[:, :], cf[:, 0:256], r=[r_cf], w=[r_bd])
            COPY("dve", bdWb[:, :], cf[:, 256:768], r=[r_cf], w=[r_bd])
            for lc in range(2):
                DMA("sp", cf[:, 768:1280], dftL_d[lc * 128:(lc + 1) * 128, :], r=[], w=[r_cf])
                COPY("dve", dLb[:, lc, :], cf[:, 768:1280], r=[r_cf], w=[r_bd])
            mgv = mg.ap().rearrange("p g (k n) -> p g k n", n=512)

            def channel_stage(t0, TB):
                DMA("sp", sfg[:, :, 0:TB], fgT_d[:, t0:t0 + TB].rearrange("(j p) n -> p j n", p=128), r=[r_fgT], w=[r_sfg])
                ACT(sfg[:, :, 0:TB], sfg[:, :, 0:TB], AF.Silu, r=[r_sfg], w=[r_sfg])
                for g in range(4):
                    cb_ = g % 2
                    for dch in range(4):
                        pi = nextpp()
                        MMG([(pp[pi][:, 0:TB], mgv[:, g, kc, dch * 128:(dch + 1) * 128], PQT[:, kc // 4, g * 4 + kc % 4, 0:TB])
                             for kc in range(8)], r=[r_mg, r_PQT], w=[r_pp[pi]])
                        TT("dve", cst[cb_][:, dch, 0:TB], pp[pi][:, 0:TB], sfg[:, g * 4 + dch, 0:TB], ALU.mult,
                           r=[r_pp[pi], r_sfg], w=[r_cst[cb_]])
                    DMA("pool", catT_d[g * 512:(g + 1) * 512, t0:t0 + TB].rearrange("(j p) n -> p j n", p=128),
                        cst[cb_][:, :, 0:TB], r=[r_cst[cb_]], w=[r_catT])

            for (s0, sn, kind, pidx) in seqs:
                if kind == 0:
                    finv = fin_d[s0:s0 + sn, :].rearrange("(r w) c -> w r c", w=64)
                    for wp in range(32):
                        ub = wp % 2
                        wo_job()
                        for wl in range(2):
                            DMA("sp", uin[ub][wl * 64:(wl + 1) * 64, :], finv[wp * 2 + wl], r=[r_fin], w=[r_uin[ub]])
                        for g in range(4):
                            for pq in range(2):
                                pi = nextpp()
                                MMG([(pp[pi][:, :], bdRb[:, pq * 128:(pq + 1) * 128], uin[ub][:, g * 512:(g + 1) * 512])],
                                    r=[r_bd, r_uin[ub]], w=[r_pp[pi]])
                                COPY("act" if pq else "dve", pqs[ub][:, pq, g * 512:(g + 1) * 512], pp[pi][:, :],
                                     r=[r_pp[pi]], w=[r_pqs[ub]])
                        for pq in range(2):
                            pv = pq1_d[pq, 0:sn, :].rearrange("(r w) c -> w r c", w=64)
                            for wl in range(2):
                                DMA("pool", pv[wp * 2 + wl], pqs[ub][wl * 64:(wl + 1) * 64, pq, :], r=[r_pqs[ub]], w=[r_pq1])
                    for b in range(sn // 512):
                        t0 = s0 + b * 512
                        for pq in range(2):
                            DMA("sp", pq1t[:, :, pq, :], pq1_d[pq, b * 512:(b + 1) * 512, :].rearrange("(t p) c -> p t c", p=128),
                                r=[r_pq1], w=[r_pq1t])
                        for tt in range(4):
                            for c16 in range(16):
                                pi = nextpp()
                                MMG([(pp[pi][:, 0:256], pq1t[:, tt, 0, c16 * 128:(c16 + 1) * 128], bdWb[:, 0:256]),
                                     (pp[pi][:, 0:256], pq1t[:, tt, 1, c16 * 128:(c16 + 1) * 128], bdWb[:, 256:512])],
                                    r=[r_bd, r_pq1t], w=[r_pp[pi]])
                                COPY("act" if c16 % 2 else "dve", PQT[:, :, c16, tt * 128:(tt + 1) * 128],
                                     pp[pi][:, 0:256].rearrange("p (a n) -> p a n", a=2), r=[r_pp[pi]], w=[r_PQT])
                        channel_stage(t0, 512)
                else:
                    for lc in range(2):
                        DMA("sp", pq1t[:, lc, 0, :], fin_d[s0 + lc * 128:s0 + (lc + 1) * 128, :], r=[r_fin], w=[r_pq1t])
                    for c16 in range(16):
                        pi = nextpp()
                        MMG([(pp[pi][:, :], pq1t[:, lc, 0, c16 * 128:(c16 + 1) * 128], dLb[:, lc, :]) for lc in range(2)],
                            r=[r_bd, r_pq1t], w=[r_pp[pi]])
                        COPY("act" if c16 % 2 else "dve", PQT[:, :, c16, 0:256],
                             pp[pi][:, :].rearrange("p (a n) -> p a n", a=2), r=[r_pp[pi]], w=[r_PQT])
                    channel_stage(s0, 256)
            while wo_jobs or wo_pend:
                wo_job()
            P.run()

        r_o = [Res(), Res()]
        with ExitStack() as st:
            P = Phase(S)
            cur[0] = P
            Sst = sbt(st, "Sst", [128, 8, 512], F32)
            Sbf = sbt(st, "Sbf", [128, 8, 512], BF16)
            qb = [sbt(st, f"qb{i}", [128, 8, 512], BF16) for i in range(2)]
            kb = [sbt(st, f"kb{i}", [128, 8, 512], BF16) for i in range(2)]
            kdb = [sbt(st, f"kdb{i}", [128, 4, 1024], BF16) for i in range(2)]
            vb = [sbt(st, f"vb{i}", [128, 4, 2048], BF16) for i in range(2)]
            attm = [sbt(st, f"attm{i}", [128, 4, 128], BF16) for i in range(2)]
            ofs = [sbt(st, f"ofs{i}", [128, 2048], F32) for i in range(2)]
            pa = [pst(st, f"pa{i}", [128, 512], F32) for i in range(2)]
            po = [pst(st, f"po{i}", [128, 512], F32) for i in range(3)]
            pz = [pst(st, f"pz{i}", [128, 512], F32) for i in range(3)]
            r_S = [Res() for _ in range(8)]
            r_Sbf = [Res() for _ in range(8)]
            r_qb, r_kb, r_kdb, r_vb, r_attm, r_ofs, r_pa = [[Res(), Res()] for _ in range(7)]
            r_po, r_pz = [Res() for _ in range(3)], [Res() for _ in range(3)]
            cnt = {"po": 0, "pz": 0, "pa": 0, "ld": 0, "of": 0, "cast": 0}

            jobs = [(s0, sn, kind, pidx, dr) for (s0, sn, kind, pidx) in seqs for dr in range(2)]
            ctxs = []
            for (s0, sn, kind, pidx, dr) in jobs:
                nch = sn // CH
                GC = min(4, nch)
                order = list(range(nch)) if dr == 0 else list(range(nch - 1, -1, -1))
                ctxs.append({"s0": s0, "nch": nch, "GC": GC, "order": order, "ngb": nch // GC, "lb_of": {}, "dr": dr})

            def load_gb(cx, gbi):
                GC, dr, s0 = cx["GC"], cx["dr"], cx["s0"]
                chs = cx["order"][gbi * GC:(gbi + 1) * GC]
                c_lo = min(chs)
                tk = s0 + c_lo * CH
                ng = GC * CH
                b = cnt["ld"] % 2
                cnt["ld"] += 1
                DMA("sp", qb[b][:, :, 0:ng], qtT_d[dr, :, tk:tk + ng].rearrange("(c p) n -> p c n", p=128),
                    r=[r_qtT[dr]], w=[r_qb[b]])
                DMA("sp", kb[b][:, :, 0:ng], ktT_d[dr, :, tk:tk + ng].rearrange("(c p) n -> p c n", p=128),
                    r=[r_ktT[dr]], w=[r_kb[b]])
                DMA("sp", kdb[b][:, 0:GC, :], kd_d[dr, tk:tk + ng, :].rearrange("(c p) n -> p c n", p=128),
                    r=[r_kd[dr]], w=[r_kdb[b]])
                DMA("sp", vb[b][:, 0:GC, :], v_d[tk:tk + ng, :].rearrange("(c p) n -> p c n", p=128),
                    r=[r_v], w=[r_vb[b]])
                cx["lb_of"][gbi] = (b, c_lo)

            def emit_att(cx, n_i):
                ch = cx["order"][n_i]
                b, c_lo = cx["lb_of"][n_i // cx["GC"]]
                cl_ = ch - c_lo
                ab = cnt["pa"] % 2
                cnt["pa"] += 1
                for h in range(4):
                    MMG([(pa[ab][:, h * 128:(h + 1) * 128], kb[b][:, h * 2 + dk, cl_ * CH:(cl_ + 1) * CH],
                          qb[b][:, h * 2 + dk, cl_ * CH:(cl_ + 1) * CH]) for dk in range(2)],
                        r=[r_kb[b], r_qb[b]], w=[r_pa[ab]])
                return ab

            load_gb(ctxs[0], 0)
            for ji, (s0, sn, kind, pidx, dr) in enumerate(jobs):
                cx = ctxs[ji]
                nch, GC, order, ngb = cx["nch"], cx["GC"], cx["order"], cx["ngb"]
                if kind == 0:
                    src = (s0f_d if dr == 0 else s0b_d).rearrange("h (c p) v -> p (h c) v", p=128)
                    DMA("pool", Sst[:, :, :], src, w=r_S)
                else:
                    MEMSET("pool", Sst[:, :, :], 0.0, w=r_S)
                for i in range(8):
                    COPY("act" if i % 2 else "dve", Sbf[:, i, :], Sst[:, i, :], r=[r_S[i]], w=[r_Sbf[i]])
                ab_next = emit_att(cx, 0)
                for n_i in range(nch):
                    ch = order[n_i]
                    gch = s0 // CH + ch
                    tk = s0 + ch * CH
                    b, c_lo = cx["lb_of"][n_i // GC]
                    cl_ = ch - c_lo
                    ab = ab_next
                    if n_i % GC == 0 and n_i // GC + 1 < ngb:
                        load_gb(cx, n_i // GC + 1)
                    if n_i == nch - 1 and ji + 1 < len(jobs):
                        load_gb(ctxs[ji + 1], 0)
                    TT("dve", attm[ab][:, :, :], pa[ab][:, :].rearrange("p (h n) -> p h n", h=4),
                       maskb16[:, dr:dr + 1, :].to_broadcast([128, 4, 128]), ALU.mult, r=[r_pa[ab], r_mask], w=[r_attm[ab]])
                    if n_i + 1 < nch:
                        ab_next = emit_att(cx, n_i + 1)
                    ob = cnt["of"] % 2
                    cnt["of"] += 1
                    for h in range(4):
                        pob = cnt["po"] % 3
                        cnt["po"] += 1
                        MMG([(po[pob][:, :], qb[b][:, h * 2, cl_ * CH:(cl_ + 1) * CH], Sbf[:, h * 2, :]),
                             (po[pob][:, :], qb[b][:, h * 2 + 1, cl_ * CH:(cl_ + 1) * CH], Sbf[:, h * 2 + 1, :]),
                             (po[pob][:, :], attm[ab][:, h, :], vb[b][:, cl_, h * 512:(h + 1) * 512])],
                            r=[r_qb[b], r_Sbf[h * 2], r_Sbf[h * 2 + 1], r_attm[ab], r_vb[b]], w=[r_po[pob]])
                        COPY("act", ofs[ob][:, h * 512:(h + 1) * 512], po[pob][:, :], r=[r_po[pob]], w=[r_ofs[ob]])
                        for dk in range(2):
                            zb = cnt["pz"] % 3
                            cnt["pz"] += 1
                            si = h * 2 + dk
                            MMG([(pz[zb][:, :], kdb[b][:, cl_, si * 128:(si + 1) * 128], vb[b][:, cl_, h * 512:(h + 1) * 512])],
                                r=[r_kdb[b], r_vb[b]], w=[r_pz[zb]])
                            STT("dve", Sst[:, si, :], Sst[:, si, :], dec[:, dr, si, gch:gch + 1], pz[zb][:, :],
                                ALU.mult, ALU.add, r=[r_S[si], r_dec, r_pz[zb]], w=[r_S[si]])
                            ce = "dve" if (cnt["cast"] % 4 == 3) else "act"
                            cnt["cast"] += 1
                            COPY(ce, Sbf[:, si, :], Sst[:, si, :], r=[r_S[si]], w=[r_Sbf[si]])
                    DMA("pool", o_d[dr, tk:tk + CH, :], ofs[ob][:, :], r=[r_ofs[ob]], w=[r_o[dr]])
                if kind == 1:
                    dst = (nsf_d if dr == 0 else nsb_d)[pidx].rearrange("h (c p) v -> p (h c) v", p=128)
                    DMA("pool", dst, Sst[:, :, :], r=r_S)
            P.run()

        with ExitStack() as st:
            P = Phase(S)
            cur[0] = P
            NBUF = 4
            gnw = sbt(st, "gnw", [128, 2048], F32)
            oa = [sbt(st, f"oa{i}", [128, 2048], F32) for i in range(NBUF)]
            obt = [sbt(st, f"obt{i}", [128, 2048], F32) for i in range(NBUF)]
            ggt = [sbt(st, f"ggt{i}", [128, 2048], BF16) for i in range(NBUF)]
            sgt = [sbt(st, f"sgt{i}", [128, 2048], F32) for i in range(NBUF)]
            gob = [sbt(st, f"gob{i}", [128, 2048], BF16) for i in range(NBUF)]
            gost = [sbt(st, f"gost{i}", [128, 16, 128], BF16) for i in range(NBUF)]
            nst = [sbt(st, f"nst{i}", [128, 16], F32) for i in range(NBUF)]
            junk = sbt(st, "junk", [128, 512], BF16)
            ptg = [[pst(st, f"ptg{i}_{j}", [128, 1024], BF16) for j in range(2)] for i in range(2)]
            r_gnw, r_junk = Res(), Res()
            r_oa, r_obt, r_ggt, r_sgt, r_gob, r_gost, r_nst = [[Res() for _ in range(NBUF)] for _ in range(7)]
            r_ptg = [[Res(), Res()], [Res(), Res()]]
            DMA("sp", gnw[:, :], gnw_d.partition_broadcast(128), w=[r_gnw])
            def p3b_loads(ti):
                tk = ti * 128
                b = ti % NBUF
                DMA("sp", oa[b][:, :], o_d[0, tk:tk + 128, :], r=[r_o[0]], w=[r_oa[b]])
                DMA("sp", obt[b][:, :], o_d[1, tk:tk + 128, :], r=[r_o[1]], w=[r_obt[b]])
                DMA("sp", ggt[b][:, :], gg_d[tk:tk + 128, :], r=[r_gg], w=[r_ggt[b]])

            NTI = NT // 128

            def stageA1(ti):
                b = ti % NBUF
                TT("dve", oa[b][:, :], oa[b][:, :], obt[b][:, :], ALU.add, r=[r_oa[b], r_obt[b]], w=[r_oa[b]])
                for h in range(4):
                    ACT(junk[:, :], oa[b][:, h * 512:(h + 1) * 512], AF.Square, r=[r_oa[b]], w=[r_junk, r_nst[b]],
                        accum_out=nst[b][:, h:h + 1])
                ACT(sgt[b][:, :], ggt[b][:, :], AF.Silu, r=[r_ggt[b]], w=[r_sgt[b]])
                TT("pool", sgt[b][:, :], sgt[b][:, :], gnw[:, :], ALU.mult, r=[r_sgt[b], r_gnw], w=[r_sgt[b]])

            def stageA2(ti):
                b = ti % NBUF
                TS("dve", nst[b][:, 4:8], nst[b][:, 0:4], 1.0 / 512, 1e-6, ALU.mult, ALU.add, r=[r_nst[b]], w=[r_nst[b]])
                cur[0].op("act", lambda e, b=b: e.sqrt(out=nst[b][:, 8:12], in_=nst[b][:, 4:8]), reads=[r_nst[b]], writes=[r_nst[b]])
                cur[0].op("dve", lambda e, b=b: e.reciprocal(out=nst[b][:, 12:16], in_=nst[b][:, 8:12]), reads=[r_nst[b]], writes=[r_nst[b]])

            def stageB(ti):
                tk = ti * 128
                b = ti % NBUF
                for h in range(4):
                    STT("dve", gob[b][:, h * 512:(h + 1) * 512], oa[b][:, h * 512:(h + 1) * 512],
                        nst[b][:, 12 + h:13 + h], sgt[b][:, h * 512:(h + 1) * 512], ALU.mult, ALU.mult,
                        r=[r_oa[b], r_nst[b], r_sgt[b]], w=[r_gob[b]])
                pb = ti % 2
                for hf in range(2):
                    TRS([(ptg[pb][hf][:, j * 128:(j + 1) * 128], gob[b][:, (hf * 8 + j) * 128:(hf * 8 + j + 1) * 128]) for j in range(8)],
                        idb[:, :], r=[r_gob[b], r_id], w=[r_ptg[pb][hf]])
                    COPY("dve", gost[b][:, hf * 8:(hf + 1) * 8, :].rearrange("p j n -> p (j n)"), ptg[pb][hf][:, :],
                         r=[r_ptg[pb][hf]], w=[r_gost[b]])
                DMA("pool", catT_d[2048:4096, tk:tk + 128].rearrange("(j p) n -> p j n", p=128), gost[b][:, :, :],
                    r=[r_gost[b]], w=[r_catT])

            p3b_loads(0)
            if NTI > 1:
                p3b_loads(1)
            stageA1(0)
            stageA2(0)
            for ti in range(NTI):
                if ti + 2 < NTI:
                    p3b_loads(ti + 2)
                if ti + 1 < NTI:
                    stageA1(ti + 1)
                stageB(ti)
                if ti + 1 < NTI:
                    stageA2(ti + 1)
            P.run()

        mid.close()
        with ExitStack() as st:
            P = Phase(S)
            cur[0] = P
            cT = sbt(st, "cT", [128, 32, 512], BF16)
            Wo = [sbt(st, f"Wo{i}", [128, 32, 512], BF16) for i in range(2)]
            yt = [sbt(st, f"yt{i}", [128, D], F32) for i in range(4)]
            gtb = sbt(st, "gtb", [128, D], F32)
            fnwb = sbt(st, "fnwb", [128, D], F32)
            tmp = [sbt(st, f"tmp{i}", [128, 512], F32) for i in range(2)]
            jk = sbt(st, "jk", [128, 512], BF16)
            ns8 = sbt(st, "ns8", [128, 4, 8], F32)
            ns4 = sbt(st, "ns4", [128, 4], F32)
            r_ns8 = [Res() for _ in range(4)]
            pm4 = [pst(st, f"pm4_{i}", [128, 512], F32) for i in range(8)]
            r_gtb, r_fnwb, r_jk, r_ns4 = [Res() for _ in range(4)]
            r_Wo, r_tmp = [Res(), Res()], [Res(), Res()]
            r_yt = [Res() for _ in range(4)]
            r_pm4 = [Res() for _ in range(8)]
            DMA("sp", fnwb[:, :], fnw_d.partition_broadcast(128), w=[r_fnwb])
            wl = [0]
            pi4 = [0]
            last_kind = -1

            def load_Wo(cb):
                b = wl[0] % 2
                wl[0] += 1
                DMA("sp", Wo[b].ap().rearrange("p c n -> p (c n)"), woutb_d[cb], r=[r_woutb[cb]], w=[r_Wo[b]])
                return b

            r_cTp = [Res() for _ in range(2)]

            def load_cT(bj, hf):
                t0_, TB_ = blocks[bj][0], blocks[bj][1]
                lo = hf * 256
                if lo >= TB_:
                    return
                DMA("sp", cT[:, :, lo:lo + 256],
                    catT_d[:, t0_ + lo:t0_ + lo + 256].rearrange("(c p) n -> p c n", p=128),
                    r=[r_catT], w=[r_cTp[hf]])

            def load_x(bj, tt):
                t0_ = blocks[bj][0]
                DMA("pool", yt[tt][:, :], x_d[t0_ + tt * 128:t0_ + (tt + 1) * 128, :], w=[r_yt[tt]])

            wb_next = load_Wo(0)
            load_cT(0, 0)
            load_cT(0, 1)
            for tt in range(blocks[0][1] // 128):
                load_x(0, tt)
            for bi, (t0, TB, kind) in enumerate(blocks):
                NTT = TB // 128
                if kind != last_kind:
                    DMA("sp", gtb[:, :], mod_d[kind, 2 * D:3 * D].partition_broadcast(128), r=[r_mod_d], w=[r_gtb])
                    last_kind = kind
                for cb in range(8):
                    wb = wb_next
                    if cb + 1 < 8:
                        wb_next = load_Wo(cb + 1)
                    elif bi + 1 < len(blocks):
                        wb_next = load_Wo(0)
                    for tt in range(NTT):
                        pb = pi4[0] % 8
                        pi4[0] += 1
                        tb_ = pi4[0] % 2
                        MMG([(pm4[pb][:, :], cT[:, kc, tt * 128:(tt + 1) * 128], Wo[wb][:, kc, :]) for kc in range(32)],
                            r=[r_cTp[tt // 2], r_Wo[wb]], w=[r_pm4[pb]])
                        TT("dve", tmp[tb_][:, :], pm4[pb][:, :], gtb[:, cb * 512:(cb + 1) * 512], ALU.mult,
                           r=[r_pm4[pb], r_gtb], w=[r_tmp[tb_]])
                        TT("pool", yt[tt][:, cb * 512:(cb + 1) * 512], yt[tt][:, cb * 512:(cb + 1) * 512], tmp[tb_][:, :], ALU.add,
                           r=[r_tmp[tb_], r_yt[tt]], w=[r_yt[tt]])
                        ACT(jk[:, :], yt[tt][:, cb * 512:(cb + 1) * 512], AF.Square, r=[r_yt[tt]], w=[r_jk, r_ns8[tt]],
                            accum_out=ns8[:, tt, cb:cb + 1])
                        if cb == 7:
                            cur[0].op("dve", lambda e, tt=tt: e.reduce_sum(out=ns4[:, 0:1], in_=ns8[:, tt, :], axis=mybir.AxisListType.X),
                                      reads=[r_ns8[tt]], writes=[r_ns4])
                            TS("dve", ns4[:, 1:2], ns4[:, 0:1], 1.0 / D, 1e-6, ALU.mult, ALU.add, r=[r_ns4], w=[r_ns4])
                            cur[0].op("act", lambda e: e.sqrt(out=ns4[:, 2:3], in_=ns4[:, 1:2]), reads=[r_ns4], writes=[r_ns4])
                            cur[0].op("dve", lambda e: e.reciprocal(out=ns4[:, 3:4], in_=ns4[:, 2:3]), reads=[r_ns4], writes=[r_ns4])
                            STT("dve", yt[tt][:, :], yt[tt][:, :], ns4[:, 3:4], fnwb[:, :], ALU.mult, ALU.mult,
                                r=[r_yt[tt], r_ns4, r_fnwb], w=[r_yt[tt]])
                            DMA("act", y_d[t0 + tt * 128:t0 + (tt + 1) * 128, :], yt[tt][:, :], r=[r_yt[tt]], w=[r_yt[tt]])
                            if bi + 1 < len(blocks) and tt % 2 == 1:
                                load_cT(bi + 1, tt // 2)
                    if cb == 7 and bi + 1 < len(blocks):
                        for tt in range(blocks[bi + 1][1] // 128):
                            load_x(bi + 1, tt)
            P.run()
    return nc


_CACHE = {}


def core_inputs(i, inp, consts, n_prompt_per_core):
    f = lambda a: np.ascontiguousarray(a, dtype=np.float32)
    npc = n_prompt_per_core
    xs = inp["x_sample"][i].reshape(-1, D)
    xp = inp["x_prompt"][i * npc:(i + 1) * npc].reshape(-1, D)
    d = {
        "x": f(np.concatenate([xs, xp], axis=0)),
        "s0f": f(inp["state_gla_fwd"][i, 0]),
        "s0b": f(inp["state_gla_bwd"][i, 0]),
        "c2": f(np.concatenate([inp["c"][i], inp["c_ctx"]])),
        "ada_w": f(inp["ada_w"][0]),
        "ada_b": f(inp["ada_b"][0]),
        "norm_w": f(inp["norm_w"][0]),
        "w_in": f(inp["w_in"][0]),
        "w_alpha": f(inp["w_alpha"][0]),
        "b_alpha": f(inp["b_alpha"][0].reshape(-1)),
        "w_fourier": f(inp["w_fourier"][0]),
        "gla_norm_w": f(inp["gla_norm_w"][0].reshape(-1)),
        "w_out": f(inp["w_out"][0]),
        "final_norm_w": f(inp["final_norm_w"]),
    }
    d.update(consts)
    return d


def kernel(**inputs):
    inp = {k: np.asarray(v) for k, v in inputs.items()}
    n = 8
    NB = inp["x_sample"].shape[0]
    NPB = inp["x_prompt"].shape[0]
    assert NB == n and NPB % n == 0
    npc = NPB // n
    NS = inp["x_sample"].shape[1]
    consts = make_consts()
    key = (NS, npc)
    if key not in _CACHE:
        _CACHE[key] = build(NS, npc)
    nc = _CACHE[key]
    in_maps = [core_inputs(i, inp, consts, npc) for i in range(n)]
    res = run_bass_kernel_spmd(nc, in_maps, core_ids=list(range(n)))
    outs = res.results
    y_s = np.stack([outs[i]["y"][:NS] for i in range(n)], axis=0).astype(np.float32)
    y_p = np.concatenate([outs[i]["y"][NS:].reshape(npc, 256, D) for i in range(n)], axis=0).astype(np.float32)
    nsf = np.concatenate([outs[i]["nsf"][:npc] for i in range(n)], axis=0)[:, None].astype(np.float32)
    nsb = np.concatenate([outs[i]["nsb"][:npc] for i in range(n)], axis=0)[:, None].astype(np.float32)
    return (y_p, y_s, nsf, nsb)
```
